# Optimizing a Trainium2 kernel written in Bass

```python
import jax, jax.numpy as jnp
from jax import lax
import numpy as np

D_MODEL = 2048
BATCH = 4
SEQ = 2048
DEPTH = 1
DEC_BATCH = 128
DEC_SEQ = 4
PAST_LEN = 16384
PAGE_SIZE = 128

D_A = D_MODEL // 2
D_B = D_MODEL - D_A
DK = 128
H_A = D_A // DK
DV = D_A // H_A
CONV_W = 3
N_META = 16
CHUNK = 64
D_FF = -(-8 * D_MODEL // (3 * 256)) * 256
ALPHA = (2 * DEPTH) ** 0.25
BETA = (8 * DEPTH) ** -0.25
LN_EPS = 1e-5
RMS_EPS = 1e-6
N_IN = 2 * H_A * DK + 2 * H_A * DV + 3 * D_B

kernel_name = "hgrn2_shortconv_hybrid_step"


def _layernorm(x, g, b):
    xf = x.astype(jnp.float32)
    mu = jnp.mean(xf, axis=-1, keepdims=True)
    var = jnp.mean(jnp.square(xf - mu), axis=-1, keepdims=True)
    return ((xf - mu) * lax.rsqrt(var + LN_EPS) * g.astype(jnp.float32) + b.astype(jnp.float32)).astype(x.dtype)


def _gla_chunked(q, k, v, logf, S0):
    B, L, H, _ = q.shape
    c = min(CHUNK, L)
    n = -(-L // c)
    pad = n * c - L

    def prep(t):
        t = jnp.pad(t, ((0, 0), (0, pad), (0, 0), (0, 0)))
        return t.reshape(B, n, c, H, t.shape[-1]).transpose(1, 0, 3, 2, 4)

    qs, ks, vs, gs = prep(q), prep(k), prep(v), prep(logf)
    causal = jnp.tril(jnp.ones((c, c), dtype=bool))
    mid = c // 2

    def step(S, inp):
        qc, kc, vc, gc = inp
        b = jnp.cumsum(gc, axis=2)
        b_mid = b[:, :, mid:mid + 1]
        b_last = b[:, :, -1:]
        a = jnp.einsum('bhtk,bhsk->bhts', qc * jnp.exp(b - b_mid), kc * jnp.exp(b_mid - b))
        a = jnp.where(causal, a, 0.0)
        o = (jnp.einsum('bhts,bhsv->bhtv', a, vc)
             + jnp.einsum('bhtk,bhkv->bhtv', qc * jnp.exp(b), S))
        S = (jnp.exp(b_last[:, :, 0, :, None]) * S
             + jnp.einsum('bhsk,bhsv->bhkv', kc * jnp.exp(b_last - b), vc))
        return S, o

    S, o = lax.scan(step, S0.astype(jnp.float32), (qs, ks, vs, gs))
    o = o.transpose(1, 0, 3, 2, 4).reshape(B, n * c, H, DV)[:, :L]
    return o, S


def _layer(x, S0, buf, n_lead, lb, w_in, b_f, gnorm_g, conv_w, w_o,
           ln1_g, ln1_b, w_gate, w_up, w_down, ln2_g, ln2_b):
    f32 = jnp.float32
    Bsz, L, _ = x.shape
    proj = x @ w_in
    cuts = np.cumsum([H_A * DK, H_A * DK, H_A * DV, H_A * DV, D_B, D_B]).tolist()
    q, zf, v, g, gate_b, gate_c, xin = jnp.split(proj, cuts, axis=-1)

    zf = zf.astype(f32) + b_f.astype(f32)
    logf = jnp.logaddexp(jnp.log(lb), jnp.log1p(-lb) + jax.nn.log_sigmoid(zf))
    k = (1.0 - lb) * jax.nn.sigmoid(-zf)
    heads = lambda t: t.astype(f32).reshape(Bsz, L, H_A, -1)
    qh, kh, vh, gh = heads(jax.nn.silu(q.astype(f32))), heads(k), heads(v), heads(logf)
    if n_lead:
        o1, S1 = _gla_chunked(qh[:, :n_lead], kh[:, :n_lead], vh[:, :n_lead], gh[:, :n_lead], S0)
        o2, S = _gla_chunked(qh[:, n_lead:], kh[:, n_lead:], vh[:, n_lead:], gh[:, n_lead:], S1)
        o = jnp.concatenate([o1, o2], axis=1)
    else:
        o, S = _gla_chunked(qh, kh, vh, gh, S0)
    o = o * lax.rsqrt(jnp.mean(jnp.square(o), axis=-1, keepdims=True) + RMS_EPS)
    o_a = (o.reshape(Bsz, L, H_A * DV) * gnorm_g.astype(f32) * jax.nn.silu(g.astype(f32))).astype(x.dtype)

    u = gate_c * xin
    full = jnp.concatenate([buf.astype(u.dtype), u], axis=1)
    conv = sum(conv_w[j] * full[:, j:j + L] for j in range(CONV_W))
    y_b = gate_b * conv
    new_buf = full[:, -(CONV_W - 1):]

    mix = jnp.concatenate([o_a, y_b], axis=-1) @ w_o
    h = _layernorm(ALPHA * x + mix, ln1_g, ln1_b)
    ff = (jax.nn.silu(h @ w_gate) * (h @ w_up)) @ w_down
    out = _layernorm(ALPHA * h + ff, ln2_g, ln2_b)
    return out, S, new_buf


def setup_inputs(seed: int = 0) -> dict:
    key = jax.random.key(seed)
    ks = jax.random.split(key, 24)
    nrm = lambda i, shape: jax.random.normal(ks[i], shape, jnp.float32)
    col_scale = jnp.concatenate([
        jnp.ones((2 * H_A * DK,)),
        jnp.full((H_A * DV,), BETA),
        jnp.ones((H_A * DV + 2 * D_B,)),
        jnp.full((D_B,), BETA),
    ]).astype(jnp.float32)
    return {
        "x_prompt": nrm(0, (BATCH, SEQ, D_MODEL)),
        "x_sample": nrm(1, (DEC_BATCH, DEC_SEQ, D_MODEL)),
        "state_hgrn": 0.5 * nrm(2, (DEPTH, DEC_BATCH, H_A, DK, DV)),
        "state_conv": nrm(3, (DEPTH, DEC_BATCH, CONV_W - 1, D_B)),
        "meta_tokens": nrm(4, (N_META, D_MODEL)),
        "ln0_g": 1.0 + 0.02 * nrm(5, (D_MODEL,)),
        "ln0_b": 0.02 * nrm(6, (D_MODEL,)),
        "w_in": nrm(7, (DEPTH, D_MODEL, N_IN)) * D_MODEL ** -0.5 * col_scale,
        "b_f": 0.1 * nrm(8, (DEPTH, H_A * DK)),
        "lb_param": 0.1 * nrm(9, (DEPTH + 1, H_A * DK)),
        "gnorm_g": 1.0 + 0.02 * nrm(10, (DEPTH, H_A * DV)),
        "conv_w": nrm(11, (DEPTH, CONV_W, D_B)) * CONV_W ** -0.5,
        "w_o": nrm(12, (DEPTH, D_MODEL, D_MODEL)) * D_MODEL ** -0.5 * BETA,
        "ln1_g": 1.0 + 0.02 * nrm(13, (DEPTH, D_MODEL)),
        "ln1_b": 0.02 * nrm(14, (DEPTH, D_MODEL)),
        "w_gate": nrm(15, (DEPTH, D_MODEL, D_FF)) * D_MODEL ** -0.5,
        "w_up": nrm(16, (DEPTH, D_MODEL, D_FF)) * D_MODEL ** -0.5,
        "w_down": nrm(17, (DEPTH, D_FF, D_MODEL)) * D_FF ** -0.5 * BETA,
        "ln2_g": 1.0 + 0.02 * nrm(18, (DEPTH, D_MODEL)),
        "ln2_b": 0.02 * nrm(19, (DEPTH, D_MODEL)),
    }


def reference(x_prompt, x_sample, state_hgrn, state_conv, meta_tokens, ln0_g, ln0_b,
              w_in, b_f, lb_param, gnorm_g, conv_w, w_o, ln1_g, ln1_b,
              w_gate, w_up, w_down, ln2_g, ln2_b):
    f32 = jnp.float32
    lbs = jnp.cumsum(jax.nn.softmax(lb_param.astype(f32), axis=0), axis=0)

    meta = jnp.broadcast_to(meta_tokens.astype(x_prompt.dtype)[None], (x_prompt.shape[0], N_META, D_MODEL))
    xp = _layernorm(jnp.concatenate([meta, x_prompt], axis=1), ln0_g, ln0_b)
    xs = _layernorm(x_sample, ln0_g, ln0_b)
    bp = x_prompt.shape[0]

    hp, cp, hs, cs = [], [], [], []
    for l in range(DEPTH):
        w = (lbs[l], w_in[l], b_f[l], gnorm_g[l], conv_w[l], w_o[l],
             ln1_g[l], ln1_b[l], w_gate[l], w_up[l], w_down[l], ln2_g[l], ln2_b[l])
        S0p = jnp.zeros((bp, H_A, DK, DV), f32)
        buf0p = jnp.zeros((bp, CONV_W - 1, D_B), xp.dtype)
        xp, Sp, bufp = _layer(xp, S0p, buf0p, N_META, *w)
        xs, Ss, bufs = _layer(xs, state_hgrn[l], state_conv[l], 0, *w)
        hp.append(Sp); cp.append(bufp); hs.append(Ss); cs.append(bufs)

    y_prompt = xp[:, N_META:]
    y_sample = xs
    return (y_prompt, y_sample, jnp.stack(hp), jnp.stack(cp), jnp.stack(hs), jnp.stack(cs))
```

```python
import os
import numpy as np
from contextlib import ExitStack
import concourse.bass as bass
import concourse.mybir as mybir
from concourse.bass_utils import run_bass_kernel_spmd

F32 = mybir.dt.float32
BF16 = mybir.dt.bfloat16
AF = mybir.ActivationFunctionType
ALU = mybir.AluOpType

D = 2048
KC = 16
NH = 8
DFF = 5632
NIN = 7168
ALPHA = 2.0 ** 0.25
LN_EPS = 1e-5
RMS_EPS = 1e-6
TO = 1090
OC = 2
SC = 1026
TP = 1040
NSLOT = 3
SLOTW = 4096
OWN_GROUPS = [(0, 512), (512, 1024), (1024, 1090)]
PRE_GROUPS = [(0, 512), (512, 1024), (1024, 1040)]
H_GROUPS = [(2, 514), (514, 1026), (1026, 1090)]
OWN_TILES = [(OC + 128 * j, 128) for j in range(8)] + [(SC, 64)]
PRE_TILES = [(0, 16)] + [(16 + 128 * j, 128) for j in range(8)]

C_ID = 0
C_MASK = 128
C_SMASK = 256
C_SEL = 320
C_FLAG = 336
C_ONES = 337
NCONST = 337 + 128


_STOP = int(os.environ.get("KSTOP", "99"))
_SUB = int(os.environ.get("KSUB", "99"))


class _Skip(Exception):
    pass


class Prog:
    ENG = ("pe", "act", "dve", "pool", "sp")

    def __init__(self):
        self.streams = {e: [] for e in self.ENG}
        self.cnt = {e: 0 for e in self.ENG}
        self.dcnt = {}
        self.waited = {e: {} for e in self.ENG}
        self.res = {}
        self.pending = {e: {} for e in self.ENG}
        self.esem = {}

    def _deps(self, reads, writes):
        need = {}
        def add(t):
            if t is None:
                return
            k, v = t
            if need.get(k, 0) < v:
                need[k] = v
        for r in reads:
            st = self.res.get(r)
            if st:
                add(st["w"])
        for w in writes:
            st = self.res.get(w)
            if st:
                add(st["w"])
                for t in st["r"]:
                    add(t)
        return need

    def op(self, eng, fn, reads=(), writes=(), dma=None):
        def _exp(lst):
            out = []
            for r in lst:
                out.append(r)
                if isinstance(r, tuple) and len(r) == 2 and r[0] == "t":
                    out.append(("t", r[1], 0)); out.append(("t", r[1], 1))
            return out
        reads = _exp(reads); writes = _exp(writes)
        psr = [r for r in reads if isinstance(r, tuple) and r[0] == "ps"]
        if psr:
            reads = [r for r in reads if not (isinstance(r, tuple) and r[0] == "ps")]
            writes = list(writes) + psr
        need = self._deps(reads, writes)
        if dma is not None and self.dcnt.get(dma, 0) > 0:
            if need.get(("d", dma), 0) < self.dcnt[dma]:
                need[("d", dma)] = self.dcnt[dma]
        for k, v in self.pending[eng].items():
            if need.get(k, 0) < v:
                need[k] = v
        self.pending[eng] = {}
        waits = []
        for k, v in need.items():
            if self.waited[eng].get(k, 0) < v:
                self.waited[eng][k] = v
                waits.append((k, v))
        if dma is None:
            self.cnt[eng] += 1
            tk = (("e", eng), self.cnt[eng])
            inc = 1
        else:
            self.dcnt[dma] = self.dcnt.get(dma, 0) + 16
            tk = (("d", dma), self.dcnt[dma])
            inc = 16
        self.streams[eng].append((waits, fn, tk[0], inc))
        for r in reads:
            self.res.setdefault(r, {"w": None, "r": []})["r"].append(tk)
        for w in writes:
            self.res[w] = {"w": tk, "r": []}
        return tk

    def barrier(self):
        allt = {}
        for e in self.ENG:
            if self.cnt[e]:
                allt[("e", e)] = self.cnt[e]
        for d, v in self.dcnt.items():
            if v:
                allt[("d", d)] = v
        for e in self.ENG:
            for k, v in allt.items():
                if self.pending[e].get(k, 0) < v:
                    self.pending[e][k] = v
        self.res = {}


def build_nc():
    nc = bass.Bass("TRN2", target_bir_lowering=False)
    P = Prog()

    def din(name, shape):
        return nc.dram_tensor(name, shape, F32, kind="ExternalInput").ap()

    def dout(name, shape):
        return nc.dram_tensor(name, shape, F32, kind="ExternalOutput").ap()

    xo = din("xo", [1024, D]); xs = din("xs", [66, D]); xp = din("xp", [TP, D])
    sh = din("sh", [16, NH, 128, 128]); scin_d = din("sc", [32, 1024])
    w_in = din("w_in", [D, NIN]); w_o = din("w_o", [D, D])
    w_gate = din("w_gate", [D, DFF]); w_up = din("w_up", [D, DFF]); w_down = din("w_down", [DFF, D])
    ln0_g = din("ln0_g", [D]); ln0_b = din("ln0_b", [D]); ln1_g = din("ln1_g", [D]); ln1_b = din("ln1_b", [D])
    ln2_g = din("ln2_g", [D]); ln2_b = din("ln2_b", [D])
    consts_d = din("consts", [128, NCONST])
    vecs_d = din("vecs", [120, 128])
    y_own = dout("y_own", [1024, D]); y_samp = dout("y_samp", [64, D])
    s_p = dout("s_p", [NH, 128, 128]); s_s = dout("s_s", [16, NH, 128, 128])
    c_p = dout("c_p", [2, 1024]); c_s = dout("c_s", [32, 1024])

    es = ExitStack()
    with es:
        def sb(name, shape, dt=F32):
            return es.enter_context(nc.sbuf_tensor(name, shape, dt))

        arA = sb("arA", [128, KC * TO], BF16)
        arB = sb("arB", [128, KC * TO], BF16)
        NC_W = 17564
        arC = sb("arC", [128, NC_W], F32)
        slots = sb("slots", [128, NSLOT, SLOTW], BF16)
        misc = sb("misc", [128, 5120], F32)
        rsamp = sb("rsamp", [128, D], F32)
        Sf = sb("Sf", [128, NH, 128], F32)
        Sb = sb("Sb", [128, NH, 128], BF16)
        cst = sb("cst", [128, NCONST], F32)
        identb = sb("identb", [128, 128], BF16)
        mask_own = sb("mask_own", [128, TO], BF16)
        mask_pre = sb("mask_pre", [128, TP], BF16)
        cols = sb("cols", [128, 144], F32)
        vs_all = sb("vs_all", [128, 1024], BF16)
        lp = sb("lp", [128, 2, NH], F32)
        C1o = sb("C1o", [128, NH, 8], F32); C2o = sb("C2o", [128, NH, 8], F32)
        C1p = sb("C1p", [128, NH, 9], F32); C2p = sb("C2p", [128, NH, 9], F32)
        C2s = sb("C2s", [128, NH, 16], F32)
        mvs = sb("mvs", [128, 9, 2], F32)
        stt_ = [sb("stt%d" % i, [128, 4, 6], F32) for i in range(4)]
        mv_ = [sb("mv%d" % i, [128, 2], F32) for i in range(4)]
        sm_ = [sb("sm%d" % i, [128, 4], F32) for i in range(4)]
        ulast = sb("ulast", [128, 8, 2], F32)
        ps = es.enter_context(nc.psum_tensor("ps", [128, 8, 512], F32))
        psb = ps[:].bitcast(BF16)

        xnT_pre = arA[:, 0:KC * TP].rearrange("p (k t) -> p k t", k=KC)
        aT = arA[:].rearrange("p (k t) -> p k t", k=KC)
        xnT = arB[:].rearrange("p (k t) -> p k t", k=KC)
        hT = xnT

        def cview(off_w, nbf16, dt=BF16):
            a = arC[:, off_w:off_w + nbf16 // 2]
            return a.bitcast(BF16)

        ka_pre = cview(0, NH * TP).rearrange("p (h t) -> p h t", h=NH)
        v_pre = cview(4160, 9 * 1024).rearrange("p (j c) -> p j c", j=9)
        qa = cview(0, 4 * TO).rearrange("p (h t) -> p h t", h=4)
        qb = cview(2180, 4 * TO).rearrange("p (h t) -> p h t", h=4)
        ka = cview(4360, 4 * TO).rearrange("p (h t) -> p h t", h=4)
        v_tok = cview(6540, 9 * 512).rearrange("p (j c) -> p j c", j=9)
        DOFF = 8844
        xst = [arC[:, DOFF:DOFF + 2048], arC[:, DOFF + 2048:DOFF + 4096],
               arC[:, DOFF + 4360:DOFF + 4360 + 2048], arC[:, DOFF + 4360 + 2048:DOFF + 4360 + 4096]]
        g2 = arC[:, DOFF:DOFF + 4 * TO].rearrange("p (h t) -> p h t", h=4)
        EOFF = DOFF + 4360
        tt = [arC[:, EOFF + i * TO:EOFF + (i + 1) * TO] for i in range(4)]
        katok = tt[0][:, 0:512].bitcast(BF16).rearrange("p (h k) -> p h k", h=NH)
        ATm = tt[0][:, 512:768].bitcast(BF16).rearrange("p (h t) -> p h t", h=4)
        km = tt[0][:, 768:1024].bitcast(BF16).rearrange("p (h k) -> p h k", h=4)
        sq = tt[1][:, 0:512]
        on = tt[1][:, 512:1024]
        rs = tt[2][:, 0:512]
        tmpS = tt[3][:, 0:1024].rearrange("p (h v) -> p h v", h=NH)
        resid = [arC[:, 2048 * j:2048 * (j + 1)] for j in range(8)] + [rsamp[:]]
        Sin = [misc[:, 512 * i:512 * (i + 1)].rearrange("p (h v) -> p h v", h=4) for i in range(4)]
        Sjb = [misc[:, 2048 + 256 * i:2048 + 256 * (i + 1)].bitcast(BF16).rearrange("p (h v) -> p h v", h=4) for i in range(4)]
        lnA = misc[:, 0:2048]; lnB = misc[:, 2048:4096]
        sgt = [misc[:, 4096:4608], misc[:, 4608:5120]]
        actT = arA[:, 0:4 * TO].rearrange("p (k t) -> p k t", k=4)
        XS3 = 4 * TO + 2 * SLOTW
        Sin3 = [arA[:, XS3 + 2048 * i:XS3 + 2048 * (i + 1)].bitcast(F32).rearrange("p (h v) -> p h v", h=NH) for i in range(2)]
        km3 = Sf[:].rearrange("p h v -> p (h v)")[:, 0:512].bitcast(BF16).rearrange("p (h k) -> p h k", h=NH)

        ks_all = mask_pre[:, 0:1024].rearrange("p (h k) -> p h k", h=NH)
        vecs = misc[:, 3584:3712]
        gsq = misc[:, 3584:4096]
        gon = misc[:, 4096:4608]
        scT = misc[:, 4608:4864].rearrange("p (j s) -> p j s", j=8)
        ulast_s = misc[:, 4864:5120].rearrange("p (j s) -> p j s", j=8)
        grs = rsamp[:, 0:512]
        gtmp = rsamp[:, 512:1024].rearrange("p (h v) -> p h v", h=4)
        gkatok = [rsamp[:, 1024:1280].bitcast(BF16).rearrange("p (h k) -> p h k", h=4),
                  rsamp[:, 1280:1536].bitcast(BF16).rearrange("p (h k) -> p h k", h=4)]
        gATm = [rsamp[:, 1536:1792].bitcast(BF16).rearrange("p (h k) -> p h k", h=4),
                rsamp[:, 1792:2048].bitcast(BF16).rearrange("p (h k) -> p h k", h=4)]
        scin = rsamp[:32, 0:1024]
        cout = rsamp[:32, 1024:2048]
        cout2 = misc[:2, 2560:3584]
        ident = cst[:, C_ID:C_ID + 128]
        maskT = cst[:, C_MASK:C_MASK + 128]
        smask = cst[:, C_SMASK:C_SMASK + 64]
        sel = cst[:, C_SEL:C_SEL + 16]
        flag = cst[:, C_FLAG:C_FLAG + 1]
        ones = cst[:, C_ONES:C_ONES + 128]
        bfneg = cols[:, 0:8]; oml = cols[:, 8:16]; omlf = cols[:, 16:24]; gn = cols[:, 24:32]
        cw = cols[:, 32:56].rearrange("p (j i) -> p j i", i=3)
        g0c = cols[:, 64:80]; b0c = cols[:, 80:96]; g1c = cols[:, 96:112]; b1c = cols[:, 112:128]
        bfr = cols[:, 128:136]
        noml = cols[:, 56:64]; nomlf = cols[:, 136:144]

        sems = {}
        def getsem(key):
            if key not in sems:
                sems[key] = es.enter_context(nc.semaphore("s%d" % len(sems)))
            return sems[key]
        for e in Prog.ENG:
            getsem(("e", e))

        def DMA(eng, out, in_, sem, reads=(), writes=(), slow=False):
            def fn(e):
                if slow:
                    return e.dma_start(out=out, in_=in_, allow_slow_non_contiguous=True)
                return e.dma_start(out=out, in_=in_)
            getsem(("d", sem))
            return P.op(eng, fn, reads=reads, writes=writes, dma=sem)

        def ACT(out, in_, func, reads, writes, scale=1.0, bias=0.0):
            P.op("act", lambda e: e.activation(out=out, in_=in_, func=func, bias=bias, scale=scale),
                 reads=reads, writes=writes)

        def TT(out, in0, in1, op, reads, writes, eng="dve"):
            P.op(eng, lambda e: e.tensor_tensor(out=out, in0=in0, in1=in1, op=op), reads=reads, writes=writes)

        def TS(out, in0, s1, op0, reads, writes, s2=None, op1=None, eng="dve"):
            if op1 is None:
                P.op(eng, lambda e: e.tensor_scalar(out=out, in0=in0, scalar1=s1, scalar2=None, op0=op0),
                     reads=reads, writes=writes)
            else:
                P.op(eng, lambda e: e.tensor_scalar(out=out, in0=in0, scalar1=s1, scalar2=s2, op0=op0, op1=op1),
                     reads=reads, writes=writes)

        def STT(out, in0, scalar, in1, op0, op1, reads, writes):
            P.op("dve", lambda e: e.scalar_tensor_tensor(out=out, in0=in0, scalar=scalar, in1=in1, op0=op0, op1=op1),
                 reads=reads, writes=writes)

        def PE(fn, reads, writes):
            P.op("pe", fn, reads=reads, writes=writes)

        def PSB(b):
            return ("ps", b)

        class WS:
            def __init__(self):
                self.units = []
                self.issued = 0
                self.next_idx = 0
                self.pool = []
                self.slot_of = {}
            def add(self, ap, kind, ncols):
                self.units.append((ap, kind, ncols))
            def grow(self, new_slots):
                self.pool = list(new_slots) + self.pool
            def _view(self, flat, kind, ncols):
                if kind == "col":
                    return flat[:, 0:KC * ncols].rearrange("p (k c) -> p k c", k=KC)
                return flat[:, :].rearrange("p (k c) -> p k c", k=2)
            def _issue(self, u):
                ap, kind, ncols = self.units[u]
                flat, res = self.pool.pop(0)
                self.pool.append((flat, res))
                self.slot_of[u] = (flat, res)
                DMA("pool", self._view(flat, kind, ncols), ap.rearrange("(k p) c -> p k c", p=128), res, writes=[res])
            def acquire(self, n):
                i0 = self.next_idx
                upto = min(len(self.units), i0 + len(self.pool))
                while self.issued < upto:
                    self._issue(self.issued)
                    self.issued += 1
                self.next_idx += n
                out = []
                for u in range(i0, i0 + n):
                    ap, kind, ncols = self.units[u]
                    flat, res = self.slot_of[u]
                    out.append((self._view(flat, kind, ncols), res))
                return out

        ws = WS()
        ws.grow([(slots[:, i, :], ("slot", i)) for i in range(NSLOT)])
        for u in range(4):
            ws.add(w_in[:, 1024 + 256 * u:1024 + 256 * (u + 1)], "col", 256)
        for u in range(4):
            ws.add(w_in[:, 2048 + 256 * u:2048 + 256 * (u + 1)], "col", 256)
        for H in range(2):
            for i in range(4):
                h = 4 * H + i
                ws.add(w_in[:, 1024 + 128 * h:1024 + 128 * (h + 1)], "col", 128)
                ws.add(w_in[:, 128 * h:128 * (h + 1)], "col", 128)
            for u in range(2):
                ws.add(w_in[:, 2048 + 512 * H + 256 * u:2048 + 512 * H + 256 * (u + 1)], "col", 256)
            for u in range(2):
                ws.add(w_in[:, 3072 + 512 * H + 256 * u:3072 + 512 * H + 256 * (u + 1)], "col", 256)
            for j in range(4 * H, 4 * H + 4):
                ws.add(w_in[:, 5120 + 128 * j:5120 + 128 * (j + 1)], "col", 128)
                ws.add(w_in[:, 6144 + 128 * j:6144 + 128 * (j + 1)], "col", 128)
                ws.add(w_in[:, 4096 + 128 * j:4096 + 128 * (j + 1)], "col", 128)
        for n in range(8):
            ws.add(w_o[:, 256 * n:256 * (n + 1)], "col", 256)
        for g in range(11):
            for u in range(2):
                ws.add(w_gate[:, 512 * g + 256 * u:512 * g + 256 * (u + 1)], "col", 256)
                ws.add(w_up[:, 512 * g + 256 * u:512 * g + 256 * (u + 1)], "col", 256)
            for u in range(2):
                ws.add(w_down[512 * g + 256 * u:512 * g + 256 * (u + 1), :], "row", 2048)

        bg = []
        def tick(n=1):
            for _ in range(n):
                for g_ in list(bg):
                    try:
                        next(g_)
                    except StopIteration:
                        bg.remove(g_)
        def drain():
            while bg:
                tick()

        try:
            if _STOP < -2:
                raise _Skip()
            nsm = [0]
            def small_load(out, in_, writes, slow=True):
                nsm[0] += 1
                DMA("act", out, in_, ("c", nsm[0]), writes=writes, slow=slow)

            small_load(cst[:], consts_d, ["cst"], slow=False)
            r_bf, r_gn, r_g0, r_b0, r_g1, r_b1, r_cw, r_lp = 0, 8, 16, 32, 48, 64, 80, 104
            vrow = [120]
            DMA("sp", vecs[0:120, :], vecs_d, ("c", 99), writes=["vecs"])
            NV = vrow[0]
            PE(lambda e: e.transpose(out=ps[:, 1, 0:NV], in_=vecs[0:NV, :], identity=ident[:NV, :NV]), ["vecs", "cst"], [PSB(1)])
            def vcopy(dst, r0, n):
                P.op("dve", lambda e: e.tensor_copy(out=dst, in_=ps[:, 1, r0:r0 + n]), reads=[PSB(1)], writes=["cols"])
            vcopy(bfr, r_bf, 8); vcopy(gn, r_gn, 8); vcopy(g0c, r_g0, 16); vcopy(b0c, r_b0, 16)
            vcopy(g1c, r_g1, 16); vcopy(b1c, r_b1, 16)
            for i in range(3):
                P.op("dve", lambda e, i=i: e.tensor_copy(out=cw[:, :, i], in_=ps[:, 1, r_cw + 8 * i:r_cw + 8 * i + 8]), reads=[PSB(1)], writes=["cols"])
            P.op("dve", lambda e: e.tensor_copy(out=lp[:].rearrange("p r h -> p (r h)"), in_=ps[:, 1, r_lp:r_lp + 16]), reads=[PSB(1)], writes=["lp"])
            small_load(scin, scin_d, ["scin"], slow=False)

            if _STOP < -1:
                raise _Skip()
            P.op("dve", lambda e: e.tensor_copy(out=identb[:], in_=ident), reads=["cst"], writes=["identb"])
            TS(bfneg, bfr, -1.0, ALU.mult, ["cols"], ["cols"])
            TT(oml, lp[:, 1, :], lp[:, 0, :], ALU.subtract, ["lp", "cols"], ["cols"])
            ACT(oml, oml, AF.Sigmoid, ["cols"], ["cols"])
            TS(omlf, oml, flag, ALU.mult, ["cols", "cst"], ["cols"])
            TS(noml, oml, -1.0, ALU.mult, ["cols"], ["cols"])
            TS(nomlf, omlf, -1.0, ALU.mult, ["cols"], ["cols"])
            P.op("pool", lambda e: e.memset(mask_own[:], 1.0), writes=["mask_own"])
            P.op("pool", lambda e: e.memset(mask_own[:, OC:OC + 1024].rearrange("p (c t) -> p c t", t=128)[:, :, 0:1], 0.0),
                 writes=["mask_own"])
            P.op("pool", lambda e: e.memset(mask_own[:, SC:SC + 64].rearrange("p (c t) -> p c t", t=4)[:, :, 0:1], 0.0),
                 writes=["mask_own"])
            P.op("pool", lambda e: e.memset(mask_own[:, 0:1], 0.0), writes=["mask_own"])
            P.op("pool", lambda e: e.memset(mask_pre[:], 1.0), writes=["mask_pre"])
            P.op("pool", lambda e: e.memset(mask_pre[:, 16:TP].rearrange("p (c t) -> p c t", t=128)[:, :, 0:1], 0.0),
                 writes=["mask_pre"])
            P.op("pool", lambda e: e.memset(mask_pre[:, 0:1], 0.0), writes=["mask_pre"])
            P.op("dve", lambda e: e.memset(Sf[:], 0.0), writes=["Sf"])
            def f_sct(e):
                ins = None
                for j in range(8):
                    ins = e.transpose(out=ps[:, 0, j * 32:(j + 1) * 32], in_=scin[:, j * 128:(j + 1) * 128], identity=ident[:32, :32])
                return ins
            PE(f_sct, ["scin", "cst"], [PSB(0)])
            ACT(scT.rearrange("p j s -> p (j s)"), ps[:, 0, 0:256], AF.Copy, [PSB(0)], ["scT"])

            if _STOP < 0:
                raise _Skip()
            stp = [0]
            def stats_rstd(src, R, res):
                q = stp[0] % 4; stp[0] += 1
                stt, mv, sm = stt_[q], mv_[q], sm_[q]
                def f(e):
                    ins = None
                    for i in range(4):
                        ins = e.bn_stats(out=stt[:R, i, :], in_=src[:R, i * 512:(i + 1) * 512])
                    return ins
                P.op("dve", f, reads=[res], writes=[("stt", q)])
                P.op("dve", lambda e: e.bn_aggr(out=mv[:R, :], in_=stt[:R].rearrange("p a b -> p (a b)")),
                     reads=[("stt", q)], writes=[("mv", q)])
                ACT(sm[:R, 2:3], mv[:R, 1:2], AF.Ln, [("mv", q)], [("sm", q)], bias=LN_EPS)
                ACT(sm[:R, 0:1], sm[:R, 2:3], AF.Exp, [("sm", q)], [("sm", q)], scale=-0.5)
                STT(sm[:R, 1:2], mv[:R, 0:1], -1.0, sm[:R, 0:1], ALU.mult, ALU.mult, [("mv", q), ("sm", q)], [("sm", q)])
                return sm, ("sm", q)

            tile_ctr = [0]
            nxs = [4]
            def ln0_tile(src_ap, R, dstT, xres, pieces, gc, bc, save_idx=None, bankset=None):
                ti = tile_ctr[0]; tile_ctr[0] += 1
                sl = ti % nxs[0]
                xt = xst[sl]
                alias = {2: [("t", 0), ("t", 1)], 3: [("t", 2), ("t", 3)]}.get(sl, [])
                DMA("sp", xt[:R], src_ap, ("xst", sl), writes=[("xst", sl)] + alias)
                sm, smres = stats_rstd(xt, R, ("xst", sl))
                if save_idx is not None:
                    P.op("dve", lambda e: e.tensor_copy(out=mvs[:R, save_idx, :], in_=sm[:R, 0:2]), reads=[smres], writes=["mvs"])
                TS(xt[:R], xt[:R], sm[:R, 0:1], ALU.mult, [("xst", sl), smres], [("xst", sl)], s2=sm[:R, 1:2], op1=ALU.add, eng="pool")
                yield
                b0 = 4 * (ti % 2) if bankset is None else bankset
                def f(e):
                    ins = None
                    for kb in range(KC):
                        ins = e.transpose(out=ps[:, b0 + kb // 4, (kb % 4) * 128:(kb % 4) * 128 + R],
                                          in_=xt[:R, kb * 128:(kb + 1) * 128], identity=ident[:R, :R])
                    return ins
                PE(f, [("xst", sl), "cst"], [PSB(b0 + i) for i in range(4)])
                yield
                for kb in range(KC):
                    for (r0, nr, c0) in pieces:
                        src = ps[:, b0 + kb // 4, (kb % 4) * 128 + r0:(kb % 4) * 128 + r0 + nr]
                        dst = dstT[:, kb, c0:c0 + nr]
                        if (kb // 4) % 2 == 0:
                            ACT(dst, src, AF.Identity, [PSB(b0 + kb // 4), "cols"], [(xres, 0)], scale=gc[:, kb:kb + 1], bias=bc[:, kb:kb + 1])
                        else:
                            TS(dst, src, gc[:, kb:kb + 1], ALU.mult, [PSB(b0 + kb // 4), "cols"], [(xres, 1)],
                               s2=bc[:, kb:kb + 1], op1=ALU.add)

            def run_gen(g_):
                for _ in g_:
                    pass

            pg = [ln0_tile(xp[c0:c0 + R, :], R, xnT_pre, "xTp", [(0, R, c0)], g0c, b0c) for (c0, R) in PRE_TILES]
            for s_ in range(len(pg) + 2):
                if s_ < len(pg):
                    next(pg[s_])
                if 0 <= s_ - 1 < len(pg):
                    next(pg[s_ - 1])
                if 0 <= s_ - 2 < len(pg):
                    run_gen(pg[s_ - 2])

            def p0_own():
                tile_ctr[0] = 12
                gens = [ln0_tile(xo[128 * j:128 * (j + 1), :], 128, xnT, "xT", [(0, 128, OC + 128 * j)], g0c, b0c,
                                 save_idx=j, bankset=4) for j in range(8)]
                gens.append(ln0_tile(xs[:, :], 66, xnT, "xT", [(0, 64, SC), (64, 2, 0)], g0c, b0c, save_idx=8, bankset=4))
                next(gens[0]); next(gens[1])
                yield
                for j in range(9):
                    run_gen(gens[j])
                    if j + 2 < 9:
                        next(gens[j + 2])
                    yield
            bg.append(p0_own())

            if _STOP < 1:
                drain()
                raise _Skip()
            blk_ctr = [0]
            fm_mode = ["low"]
            def fm_block(wv, wres, cb, xT, xres, groups):
                bi = blk_ctr[0]; blk_ctr[0] += 1
                if fm_mode[0] == "dbl":
                    banks = [3 * (bi % 2) + i for i in range(3)]
                else:
                    banks = [0, 1, 2]
                def f(e):
                    ins = None
                    for k in range(KC):
                        for gi, (c0, c1) in enumerate(groups):
                            ins = e.matmul(ps[:, banks[gi], 0:c1 - c0], lhsT=wv[:, k, cb:cb + 128], rhs=xT[:, k, c0:c1],
                                           start=(k == 0), stop=(k == KC - 1))
                    return ins
                PE(f, [wres, (xres, 0), (xres, 1)], [PSB(b) for b in banks])
                return banks

            def per_group(groups, banks, fn):
                for gi, (c0, c1) in enumerate(groups):
                    fn(ps[:, banks[gi], 0:c1 - c0], c0, c1, PSB(banks[gi]))

            vt_ctr = [0]
            def v_tile(units, xT, xres, c0, R, vdst, ti, col0, banks=(6, 7)):
                b = banks[vt_ctr[0] % len(banks)]; vt_ctr[0] += 1
                def f(e):
                    ins = None
                    for u, (wv, wres) in enumerate(units):
                        for k in range(KC):
                            ins = e.matmul(ps[:R, b, 256 * u:256 * (u + 1)], lhsT=xT[:, k, c0:c0 + R], rhs=wv[:, k, 0:256],
                                           start=(k == 0), stop=(k == KC - 1))
                    return ins
                PE(f, [units[0][1], units[1][1], (xres, 0), (xres, 1)], [PSB(b)])
                ACT(vdst[:R, ti, col0:col0 + 512], ps[:R, b, 0:512], AF.Copy, [PSB(b)], ["v"])

            fm_mode[0] = "dbl"
            for u in range(4):
                (wv, wres), = ws.acquire(1)
                for hh in range(2):
                    h = 2 * u + hh
                    if h == 6:
                        tick()
                    banks = fm_block(wv, wres, 128 * hh, xnT_pre, "xTp", PRE_GROUPS)
                    per_group(PRE_GROUPS, banks, lambda src, c0, c1, pres, h=h: ACT(
                        tt[0][:, c0:c1], src, AF.Sigmoid, [pres, "cols"], [("t", 0)], scale=-1.0, bias=bfneg[:, h:h + 1]))
                    ACT(tt[1][:, 0:16], tt[0][:, 0:16], AF.Ln, [("t", 0), "cols"], [("t", 1)], scale=noml[:, h:h + 1], bias=1.0)
                    ACT(tt[1][:, 16:TP], tt[0][:, 16:TP], AF.Ln, [("t", 0), "cols"], [("t", 1)], scale=nomlf[:, h:h + 1], bias=1.0)
                    P.op("dve", lambda e: e.tensor_tensor_scan(out=tt[2][:, 0:TP], data0=mask_pre[:, :], data1=tt[1][:, 0:TP],
                                                               initial=0.0, op0=ALU.mult, op1=ALU.add),
                         reads=[("t", 1), "mask_pre"], writes=[("t", 2)])
                    TT(tt[1][:, 0:16], tt[2][:, 0:16], tt[2][:, 8:9].broadcast_to([128, 16]), ALU.subtract,
                       [("t", 2)], [("t", 1)])
                    b3 = tt[2][:, 16:TP].rearrange("p (c t) -> p c t", t=128)
                    d3 = tt[1][:, 16:TP].rearrange("p (c t) -> p c t", t=128)
                    TT(d3, b3, b3[:, :, 64:65].broadcast_to([128, 8, 128]), ALU.subtract, [("t", 2)], [("t", 1)])
                    ACT(tt[3][:, 0:TP], tt[1][:, 0:TP], AF.Exp, [("t", 1)], [("t", 3)], scale=-1.0)
                    STT(ka_pre[:, h, 0:16], tt[0][:, 0:16], oml[:, h:h + 1], tt[3][:, 0:16], ALU.mult, ALU.mult,
                        [("t", 0), ("t", 3), "cols"], [("ka", h)])
                    STT(ka_pre[:, h, 16:TP], tt[0][:, 16:TP], omlf[:, h:h + 1], tt[3][:, 16:TP], ALU.mult, ALU.mult,
                        [("t", 0), ("t", 3), "cols"], [("ka", h)])
                    ACT(C2p[:, h, 0:1], tt[2][:, 15:16], AF.Exp, [("t", 2)], ["C2p"])
                    ACT(C2p[:, h, 1:9], b3[:, :, 127], AF.Exp, [("t", 2)], ["C2p"])
                    ACT(C1p[:, h, 0:1], tt[1][:, 15:16], AF.Exp, [("t", 1)], ["C1p"])
                    ACT(C1p[:, h, 1:9], d3[:, :, 127], AF.Exp, [("t", 1)], ["C1p"])
            def scan_tr(c, c0, R):
                xb = 3
                def f(e):
                    ins = None
                    for h in range(NH):
                        ins = e.transpose(out=psb[:R, xb, h * 128:(h + 1) * 128], in_=ka_pre[:, h, c0:c0 + R], identity=identb[:, :])
                    return ins
                PE(f, [("ka", h) for h in range(NH)] + ["identb"], [PSB(xb)])
                ACT(katok[:R].rearrange("p h k -> p (h k)"), psb[:R, xb, 0:1024], AF.Copy, [PSB(xb)], [("t", 0)])

            def scan_upd(c, c0, R):
                pb = 4 + 2 * (c % 2)
                def f2(e):
                    ins = None
                    for h in range(NH):
                        ins = e.matmul(ps[:, pb + h // 4, (h % 4) * 128:(h % 4 + 1) * 128], lhsT=katok[:R, h, :],
                                       rhs=v_pre[:R, c, h * 128:(h + 1) * 128], start=True, stop=True)
                    return ins
                PE(f2, [("t", 0), "v"], [PSB(pb), PSB(pb + 1)])
                TT(Sf[:], Sf[:], C2p[:, :, c:c + 1].broadcast_to([128, NH, 128]), ALU.mult, ["Sf", "C2p"], ["Sf"])
                pv = ps[:, pb:pb + 2, :].rearrange("p b (h v) -> p (b h) v", v=128)
                TT(tmpS, pv, C1p[:, :, c:c + 1].broadcast_to([128, NH, 128]), ALU.mult, [PSB(pb), PSB(pb + 1), "C1p"], [("t", 3)])
                TT(Sf[:], Sf[:], tmpS, ALU.add, ["Sf", ("t", 3)], ["Sf"])

            units0 = ws.acquire(2)
            for ti, (c0, R) in enumerate(PRE_TILES):
                v_tile(units0, xnT_pre, "xTp", c0, R, v_pre, ti, 0, banks=(0, 1, 2))
                tick()
            drain()
            units1 = ws.acquire(2)
            for ti, (c0, R) in enumerate(PRE_TILES):
                v_tile(units1, xnT_pre, "xTp", c0, R, v_pre, ti, 512, banks=(0, 1, 2))
                if ti > 0:
                    scan_upd(ti - 1, *PRE_TILES[ti - 1])
                scan_tr(ti, c0, R)
            scan_upd(len(PRE_TILES) - 1, *PRE_TILES[-1])
            P.barrier()

            if _STOP < 2:
                raise _Skip()
            fm_mode[0] = "dbl"
            own3 = lambda t: t[:, OC:OC + 1024].rearrange("p (c t) -> p c t", t=128)
            smp3 = lambda t: t[:, SC:SC + 64].rearrange("p (c t) -> p c t", t=4)

            HALVES = [(0, 514), (514, TO)]
            def evac_split(banks, fn):
                fn(ps[:, banks[0], 0:512], 0, 512, PSB(banks[0]), 0)
                fn(ps[:, banks[1], 0:2], 512, 514, PSB(banks[1]), 0)
                fn(ps[:, banks[1], 2:512], 514, 1024, PSB(banks[1]), 1)
                fn(ps[:, banks[2], 0:66], 1024, 1090, PSB(banks[2]), 1)

            def head_elementwise(H, i):
                h = 4 * H + i
                T = lambda k, hf: ("t", k, hf)
                HV = list(enumerate(HALVES))
                (wv, wres), = ws.acquire(1)
                banks = fm_block(wv, wres, 0, xnT, "xT", OWN_GROUPS)
                evac_split(banks, lambda src, c0, c1, pres, hf: ACT(
                    tt[0][:, c0:c1], src, AF.Sigmoid, [pres, "cols"], [T(0, hf)], scale=-1.0, bias=bfneg[:, h:h + 1]))
                for hf, (a0, a1) in HV:
                    ACT(tt[1][:, a0:a1], tt[0][:, a0:a1], AF.Ln, [T(0, hf), "cols"], [T(1, hf)], scale=noml[:, h:h + 1], bias=1.0)
                for hf, (a0, a1) in HV:
                    P.op("dve", lambda e, a0=a0, a1=a1: e.tensor_tensor_scan(
                        out=tt[2][:, a0:a1], data0=mask_own[:, a0:a1], data1=tt[1][:, a0:a1], initial=0.0, op0=ALU.mult, op1=ALU.add),
                        reads=[T(1, hf), "mask_own"], writes=[T(2, hf)])
                bo0 = tt[2][:, OC:OC + 512].rearrange("p (c t) -> p c t", t=128)
                do0 = tt[1][:, OC:OC + 512].rearrange("p (c t) -> p c t", t=128)
                bo1 = tt[2][:, OC + 512:OC + 1024].rearrange("p (c t) -> p c t", t=128)
                do1 = tt[1][:, OC + 512:OC + 1024].rearrange("p (c t) -> p c t", t=128)
                bs = smp3(tt[2]); dsm = smp3(tt[1])
                TT(do0, bo0, bo0[:, :, 64:65].broadcast_to([128, 4, 128]), ALU.subtract, [T(2, 0)], [T(1, 0)])
                P.op("dve", lambda e: e.memset(tt[1][:, 0:2], 0.0), reads=[T(2, 0)], writes=[T(1, 0)])
                TT(do1, bo1, bo1[:, :, 64:65].broadcast_to([128, 4, 128]), ALU.subtract, [T(2, 1)], [T(1, 1)])
                TT(dsm, bs, bs[:, :, 3:4].broadcast_to([128, 16, 4]), ALU.subtract, [T(2, 1)], [T(1, 1)])
                for hf, (a0, a1) in HV:
                    ACT(tt[3][:, a0:a1], tt[1][:, a0:a1], AF.Exp, [T(1, hf)], [T(3, hf)], scale=-1.0)
                for hf, (a0, a1) in HV:
                    STT(ka[:, i, a0:a1], tt[0][:, a0:a1], oml[:, h:h + 1], tt[3][:, a0:a1], ALU.mult, ALU.mult,
                        [T(0, hf), T(3, hf), "cols"], [("ka", i)])
                ACT(C2o[:, h, 0:4], bo0[:, :, 127], AF.Exp, [T(2, 0)], ["C2o"])
                ACT(C1o[:, h, 0:4], do0[:, :, 127], AF.Exp, [T(1, 0)], ["C1o"])
                ACT(C2o[:, h, 4:8], bo1[:, :, 127], AF.Exp, [T(2, 1)], ["C2o"])
                ACT(C1o[:, h, 4:8], do1[:, :, 127], AF.Exp, [T(1, 1)], ["C1o"])
                ACT(C2s[:, h, :], bs[:, :, 3], AF.Exp, [T(2, 1)], ["C2s"])
                (wv, wres), = ws.acquire(1)
                banks = fm_block(wv, wres, 0, xnT, "xT", OWN_GROUPS)
                evac_split(banks, lambda src, c0, c1, pres, hf: ACT(
                    tt[0][:, c0:c1], src, AF.Silu, [pres], [T(0, hf)]))
                for hf, (a0, a1) in HV:
                    ACT(tt[3][:, a0:a1], tt[1][:, a0:a1], AF.Exp, [T(1, hf)], [T(3, hf)])
                for hf, (a0, a1) in HV:
                    TT(qa[:, i, a0:a1], tt[0][:, a0:a1], tt[3][:, a0:a1], ALU.mult, [T(0, hf), T(3, hf)], [("qa", i)])
                for hf, (a0, a1) in HV:
                    ACT(tt[3][:, a0:a1], tt[2][:, a0:a1], AF.Exp, [T(2, hf)], [T(3, hf)])
                for hf, (a0, a1) in HV:
                    TT(qb[:, i, a0:a1], tt[0][:, a0:a1], tt[3][:, a0:a1], ALU.mult, [T(0, hf), T(3, hf)], [("qb", i)])

            def gla_half(H):
                hs = slice(4 * H, 4 * H + 4)
                XB, YB, VB = 3, 4, 7
                P.op("act", lambda e: e.activation(out=Sb[:, hs, :], in_=Sf[:, hs, :], func=AF.Copy), reads=["Sf"], writes=["Sb"])

                def pieceA(c):
                    cc = OC + 128 * c
                    q = c % 2
                    def f1(e):
                        ins = None
                        for i in range(4):
                            ins = e.transpose(out=psb[:, XB, i * 128:(i + 1) * 128], in_=ka[:, i, cc:cc + 128], identity=identb[:, :])
                        for i in range(4):
                            ins = e.matmul(ps[:, YB, i * 128:(i + 1) * 128], lhsT=ka[:, i, cc:cc + 128], rhs=qa[:, i, cc:cc + 128],
                                           start=True, stop=True)
                        return ins
                    PE(f1, [("ka", i) for i in range(4)] + [("qa", i) for i in range(4)] + ["identb"], [PSB(XB), PSB(YB)])
                    ACT(gkatok[q][:, :, :].rearrange("p h k -> p (h k)"), psb[:, XB, 0:512], AF.Copy, [PSB(XB)], [("gk", q)])
                    TT(gATm[q][:, :, :], ps[:, YB, :].rearrange("p (h t) -> p h t", h=4),
                       maskT.rearrange("p (o t) -> p o t", o=1).broadcast_to([128, 4, 128]), ALU.mult, [PSB(YB), "cst"], [("ga", q)])

                def pieceB(c):
                    cc = OC + 128 * c
                    q = c % 2
                    zb = 5 + q
                    def f3(e):
                        ins = None
                        for i in range(4):
                            ins = e.matmul(ps[:, VB, i * 128:(i + 1) * 128], lhsT=gkatok[q][:, i, :], rhs=v_tok[:, c, i * 128:(i + 1) * 128],
                                           start=True, stop=True)
                        return ins
                    PE(f3, [("gk", q), "v"], [PSB(VB)])
                    def f2(e):
                        ins = None
                        for i in range(4):
                            e.matmul(ps[:, zb, i * 128:(i + 1) * 128], lhsT=v_tok[:, c, i * 128:(i + 1) * 128], rhs=gATm[q][:, i, :],
                                     start=True, stop=False)
                            ins = e.matmul(ps[:, zb, i * 128:(i + 1) * 128], lhsT=Sb[:, 4 * H + i, :], rhs=qb[:, i, cc:cc + 128],
                                           start=False, stop=True)
                        return ins
                    PE(f2, [("ga", q), "v", "Sb"] + [("qb", i) for i in range(4)], [PSB(zb)])
                    ACT(gsq[:, 0:512], ps[:, zb, 0:512], AF.Square, [PSB(zb)], ["gsq"])
                    TT(Sf[:, hs, :], Sf[:, hs, :], C2o[:, hs, c:c + 1].broadcast_to([128, 4, 128]), ALU.mult, ["Sf", "C2o"], ["Sf"])
                    TT(gtmp[:, :, :], ps[:, VB, :].rearrange("p (h v) -> p h v", h=4),
                       C1o[:, hs, c:c + 1].broadcast_to([128, 4, 128]), ALU.mult, [PSB(VB), "C1o"], ["gtmp"])
                    TT(Sf[:, hs, :], Sf[:, hs, :], gtmp[:, :, :], ALU.add, ["Sf", "gtmp"], ["Sf"])
                    if c < 7:
                        P.op("act", lambda e: e.activation(out=Sb[:, hs, :], in_=Sf[:, hs, :], func=AF.Copy), reads=["Sf"], writes=["Sb"])

                def pieceC(zb, ncol, dst, gv, hsplit, hw):
                    PE(lambda e: e.matmul(ps[:, YB, 0:ncol], lhsT=ones, rhs=gsq[:, 0:ncol], start=True, stop=True),
                       ["gsq", "cst"], [PSB(YB)])
                    ACT(grs[:, 0:ncol], ps[:, YB, 0:ncol], AF.Ln, [PSB(YB)], ["grs"], scale=1.0 / 128.0, bias=RMS_EPS)
                    ACT(grs[:, 0:ncol], grs[:, 0:ncol], AF.Exp, ["grs"], ["grs"], scale=-0.5)
                    TT(gon[:, 0:ncol], ps[:, zb, 0:ncol], grs[:, 0:ncol], ALU.mult, [PSB(zb), "grs"], ["gon"])
                    TT(dst, gon[:, 0:ncol].rearrange("p (h t) -> p h t", h=4), gv, ALU.mult, ["gon", "g2"], ["aT"])

                pieceA(0)
                yield
                for c in range(8):
                    if c > 0:
                        cp_ = c - 1
                        ccp = OC + 128 * cp_
                        pieceC(5 + cp_ % 2, 512, aT[:, 4 * H:4 * H + 4, ccp:ccp + 128], g2[:, :, ccp:ccp + 128], 4, 128)
                    pieceB(c)
                    if c + 1 < 8:
                        pieceA(c + 1)
                    yield
                ccp = OC + 128 * 7
                pieceC(5 + 7 % 2, 512, aT[:, 4 * H:4 * H + 4, ccp:ccp + 128], g2[:, :, ccp:ccp + 128], 4, 128)
                DMA("sp", s_p[4 * H:4 * H + 4].rearrange("h k v -> k h v"), Sf[:, hs, :], ("sp", H), reads=["Sf"])
                zb = 5
                def g1(e):
                    ins = None
                    for i in range(4):
                        ins = e.transpose(out=psb[:64, XB, i * 128:(i + 1) * 128], in_=ka[:, i, SC:SC + 64], identity=identb[:, :])
                    for i in range(4):
                        ins = e.matmul(ps[:64, YB, i * 64:(i + 1) * 64], lhsT=ka[:, i, SC:SC + 64], rhs=qa[:, i, SC:SC + 64],
                                       start=True, stop=True)
                    return ins
                PE(g1, [("ka", i) for i in range(4)] + [("qa", i) for i in range(4)] + ["identb"], [PSB(XB), PSB(YB)])
                ACT(gkatok[0][:64, :, :].rearrange("p h k -> p (h k)"), psb[:64, XB, 0:512], AF.Copy, [PSB(XB)], [("gk", 0)])
                ATs = gATm[0][:64, :, 0:64]
                TT(ATs, ps[:64, YB, 0:256].rearrange("p (h t) -> p h t", h=4),
                   smask[:64, :].rearrange("p (o t) -> p o t", o=1).broadcast_to([64, 4, 64]), ALU.mult, [PSB(YB), "cst"], [("ga", 0)])
                yield
                def g2f(e):
                    ins = None
                    for i in range(4):
                        ins = e.matmul(ps[:, zb, i * 64:(i + 1) * 64], lhsT=v_tok[:64, 8, i * 128:(i + 1) * 128], rhs=ATs[:, i, :],
                                       start=(i == 0), stop=False, skip_group_check=True)
                    return ins
                PE(g2f, [("ga", 0), "v"], [PSB(zb)])
                P.op("dve", lambda e: e.tensor_copy(out=ks_all[:64, 4 * H:4 * H + 4, :], in_=gkatok[0][:64, :, :]),
                     reads=[("gk", 0)], writes=["ks_all"])
                P.op("dve", lambda e: e.tensor_copy(out=vs_all[:64, 512 * H:512 * (H + 1)], in_=v_tok[:64, 8, 0:512]),
                     reads=["v"], writes=["vs_all"])
                for j in range(16):
                    sl = j % 4
                    DMA("sp", Sin[sl], sh[j, 4 * H:4 * H + 4].rearrange("h k v -> k h v"), ("sin", sl), writes=[("Sin", sl)])
                    P.op("act", lambda e, sl=sl: e.activation(out=Sjb[sl], in_=Sin[sl], func=AF.Copy), reads=[("Sin", sl)], writes=[("Sjb", sl)])
                    def g3(e, j=j, sl=sl):
                        ins = None
                        for i in range(4):
                            ins = e.matmul(ps[:, zb, i * 64 + 4 * j:i * 64 + 4 * j + 4], lhsT=Sjb[sl][:, i, :],
                                           rhs=qb[:, i, SC + 4 * j:SC + 4 * j + 4], start=False, stop=(j == 15), skip_group_check=True)
                        return ins
                    PE(g3, [("Sjb", sl)] + [("qb", i) for i in range(4)], [PSB(zb)])
                    if j % 8 == 7:
                        yield
                ACT(gsq[:, 0:256], ps[:, zb, 0:256], AF.Square, [PSB(zb)], ["gsq"])
                pieceC(zb, 256, aT[:, 4 * H:4 * H + 4, SC:SC + 64], g2[:, :, SC:SC + 64], 4, 64)

            def conv_block(j):
                (wv, wres), = ws.acquire(1)
                banks = fm_block(wv, wres, 0, xnT, "xT", OWN_GROUPS)
                per_group(OWN_GROUPS, banks, lambda src, c0, c1, pres: ACT(tt[0][:, c0:c1], src, AF.Copy, [pres], [("t", 0)]))
                tick()
                (wv, wres), = ws.acquire(1)
                banks = fm_block(wv, wres, 0, xnT, "xT", OWN_GROUPS)
                U = tt[1]
                Us = tt[2][:, 0:96].rearrange("p (s t) -> p s t", t=6)
                TT(U[:, 0:512], ps[:, banks[0], 0:512], tt[0][:, 0:512], ALU.mult, [PSB(banks[0]), ("t", 0)], [("t", 1)])
                TT(U[:, 512:1024], ps[:, banks[1], 0:512], tt[0][:, 512:1024], ALU.mult, [PSB(banks[1]), ("t", 0)], [("t", 1)])
                TT(U[:, 1024:1026], ps[:, banks[2], 0:2], tt[0][:, 1024:1026], ALU.mult, [PSB(banks[2]), ("t", 0)], [("t", 1)])
                TT(Us[:, :, 2:6], ps[:, banks[2], 2:66].rearrange("p (s t) -> p s t", t=4),
                   tt[0][:, SC:SC + 64].rearrange("p (s t) -> p s t", t=4), ALU.mult, [PSB(banks[2]), ("t", 0)], [("t", 2)])
                P.op("dve", lambda e: e.tensor_copy(out=Us[:, :, 0:2], in_=scT[:, j, :].rearrange("p (s t) -> p s t", t=2)),
                     reads=["scT"], writes=[("t", 2)])
                tick()
                cv = tt[3]
                TS(cv[:, 0:1024], U[:, 2:1026], cw[:, j, 2:3], ALU.mult, [("t", 1), "cols"], [("t", 3)])
                STT(cv[:, 0:1024], U[:, 1:1025], cw[:, j, 1:2], cv[:, 0:1024], ALU.mult, ALU.add, [("t", 1), ("t", 3), "cols"], [("t", 3)])
                STT(cv[:, 0:1024], U[:, 0:1024], cw[:, j, 0:1], cv[:, 0:1024], ALU.mult, ALU.add, [("t", 1), ("t", 3), "cols"], [("t", 3)])
                cvs = cv[:, 1024:1088].rearrange("p (s t) -> p s t", t=4)
                TS(cvs, Us[:, :, 2:6], cw[:, j, 2:3], ALU.mult, [("t", 2), "cols"], [("t", 3)])
                STT(cvs, Us[:, :, 1:5], cw[:, j, 1:2], cvs, ALU.mult, ALU.add, [("t", 2), ("t", 3), "cols"], [("t", 3)])
                STT(cvs, Us[:, :, 0:4], cw[:, j, 0:1], cvs, ALU.mult, ALU.add, [("t", 2), ("t", 3), "cols"], [("t", 3)])
                P.op("act", lambda e: e.activation(out=ulast[:, j, :], in_=U[:, 1024:1026], func=AF.Copy), reads=[("t", 1)], writes=["ulast"])
                P.op("act", lambda e: e.activation(out=ulast_s[:, j, :].rearrange("p (s t) -> p s t", t=2), in_=Us[:, :, 4:6], func=AF.Copy),
                     reads=[("t", 2)], writes=["ulast_s"])
                (wv, wres), = ws.acquire(1)
                banks = fm_block(wv, wres, 0, xnT, "xT", OWN_GROUPS)
                TT(aT[:, 8 + j, OC:OC + 510], ps[:, banks[0], 2:512], cv[:, 0:510], ALU.mult, [PSB(banks[0]), ("t", 3)], ["aT"])
                TT(aT[:, 8 + j, OC + 510:OC + 1022], ps[:, banks[1], 0:512], cv[:, 510:1022], ALU.mult, [PSB(banks[1]), ("t", 3)], ["aT"])
                TT(aT[:, 8 + j, OC + 1022:OC + 1024], ps[:, banks[2], 0:2], cv[:, 1022:1024], ALU.mult, [PSB(banks[2]), ("t", 3)], ["aT"])
                TT(aT[:, 8 + j, SC:SC + 64], ps[:, banks[2], 2:66], cv[:, 1024:1088], ALU.mult, [PSB(banks[2]), ("t", 3)], ["aT"])
                tick()

            for H in range(2):
                fm_mode[0] = "dbl"
                for i in range(4):
                    head_elementwise(H, i)
                units = ws.acquire(2)
                for ti, (c0, R) in enumerate(OWN_TILES):
                    v_tile(units, xnT, "xT", c0, R, v_tok, ti, 0)
                for u in range(2):
                    (wv, wres), = ws.acquire(1)
                    for hh in range(2):
                        i = 2 * u + hh
                        banks = fm_block(wv, wres, 128 * hh, xnT, "xT", OWN_GROUPS)
                        per_group(OWN_GROUPS, banks, lambda src, c0, c1, pres, i=i: ACT(
                            g2[:, i, c0:c1], src, AF.Silu, [pres], ["g2"]))
                        TS(g2[:, i, :], g2[:, i, :], gn[:, 4 * H + i:4 * H + i + 1], ALU.mult, ["g2", "cols"], ["g2"])
                fm_mode[0] = "low"
                bg.append(gla_half(H))
                tick()
                for j in range(4 * H, 4 * H + 4):
                    conv_block(j)
                drain()

            def fc1(e):
                ins = None
                for j in range(8):
                    ins = e.transpose(out=ps[:2, j // 4, (j % 4) * 128:(j % 4 + 1) * 128], in_=ulast[:, j, :], identity=ident)
                return ins
            PE(fc1, ["ulast", "cst"], [PSB(0), PSB(1)])
            ACT(cout2, ps[:2, 0:2, :].rearrange("p b c -> p (b c)"), AF.Copy, [PSB(0), PSB(1)], ["cout2"])
            DMA("sp", c_p, cout2, ("cp", 0), reads=["cout2"])
            def fc2(e):
                ins = None
                for j in range(8):
                    ins = e.transpose(out=ps[:32, 2 + j // 4, (j % 4) * 128:(j % 4 + 1) * 128], in_=ulast_s[:, j, :], identity=ident)
                return ins
            PE(fc2, ["ulast_s", "cst"], [PSB(2), PSB(3)])
            ACT(cout, ps[:32, 2:4, :].rearrange("p b c -> p (b c)"), AF.Copy, [PSB(2), PSB(3)], ["cout"])
            DMA("sp", c_s, cout, ("cs", 0), reads=["cout"])
            P.barrier()

            if _STOP < 3:
                raise _Skip()
            def load_ln(gv, bv, scale):
                DMA("sp", lnA, gv.partition_broadcast(128), ("ln", 0), writes=["lnA"])
                DMA("sp", lnB, bv.partition_broadcast(128), ("ln", 1), writes=["lnB"])
                if scale != 1.0:
                    TS(lnA, lnA, scale, ALU.mult, ["lnA"], ["lnA"])
                    TS(lnB, lnB, scale, ALU.mult, ["lnB"], ["lnB"])

            load_ln(ln0_g, ln0_b, ALPHA)
            xsrc = [xo[128 * j:128 * (j + 1), :] for j in range(8)] + [xs[0:64, :]]
            for t, (c0, R) in enumerate(OWN_TILES):
                DMA("sp", resid[t][:R], xsrc[t], ("rx", t % 4), writes=[("r", t)])
            s2c = [0]
            for n in range(8):
                (wv, wres), = ws.acquire(1)
                cs = slice(256 * n, 256 * (n + 1))
                for t, (c0, R) in enumerate(OWN_TILES):
                    rt = resid[t]
                    b = s2c[0] % 4; s2c[0] += 1
                    def f(e, c0=c0, R=R, b=b, wv=wv):
                        ins = None
                        for k in range(KC):
                            ins = e.matmul(ps[:R, b, 0:256], lhsT=aT[:, k, c0:c0 + R], rhs=wv[:, k, 0:256],
                                           start=(k == 0), stop=(k == KC - 1))
                        return ins
                    PE(f, [wres, "aT"], [PSB(b)])
                    ACT(rt[:R, cs], rt[:R, cs], AF.Identity, [("r", t), "mvs"], [("r", t)], scale=mvs[:R, t, 0:1], bias=mvs[:R, t, 1:2])
                    TT(rt[:R, cs], rt[:R, cs], lnA[:R, cs], ALU.mult, [("r", t), "lnA"], [("r", t)])
                    TT(rt[:R, cs], rt[:R, cs], lnB[:R, cs], ALU.add, [("r", t), "lnB"], [("r", t)])
                    TT(rt[:R, cs], rt[:R, cs], ps[:R, b, 0:256], ALU.add, [("r", t), PSB(b)], [("r", t)])
            load_ln(ln1_g, ln1_b, ALPHA)

            P.barrier()
            def ln1_tile(t, c0, R):
                rt = resid[t]
                sm, smres = stats_rstd(rt, R, ("r", t))
                ACT(rt[:R], rt[:R], AF.Identity, [("r", t), smres], [("r", t)], scale=sm[:R, 0:1], bias=sm[:R, 1:2])
                yield
                b0 = 4 * (t % 2)
                def f(e):
                    ins = None
                    for kb in range(KC):
                        ins = e.transpose(out=ps[:, b0 + kb // 4, (kb % 4) * 128:(kb % 4) * 128 + R],
                                          in_=rt[:R, kb * 128:(kb + 1) * 128], identity=ident[:R, :R])
                    return ins
                PE(f, [("r", t), "cst"], [PSB(b0 + i) for i in range(4)])
                yield
                for kb in range(KC):
                    src = ps[:, b0 + kb // 4, (kb % 4) * 128:(kb % 4) * 128 + R]
                    dst = hT[:, kb, c0:c0 + R]
                    if (kb // 4) % 2 == 0:
                        ACT(dst, src, AF.Identity, [PSB(b0 + kb // 4), "cols"], [("hT", 0)], scale=g1c[:, kb:kb + 1], bias=b1c[:, kb:kb + 1])
                    else:
                        TS(dst, src, g1c[:, kb:kb + 1], ALU.mult, [PSB(b0 + kb // 4), "cols"], [("hT", 1)], s2=b1c[:, kb:kb + 1], op1=ALU.add)
            lg = [ln1_tile(t, c0, R) for t, (c0, R) in enumerate(OWN_TILES)]
            for s_ in range(len(lg) + 2):
                if s_ < len(lg):
                    next(lg[s_])
                if 0 <= s_ - 1 < len(lg):
                    next(lg[s_ - 1])
                if 0 <= s_ - 2 < len(lg):
                    run_gen(lg[s_ - 2])
            if _STOP < 4:
                raise _Skip()
            ws.grow([(arA[:, 4 * TO + i * SLOTW:4 * TO + (i + 1) * SLOTW], ("xslot", i)) for i in range(2)])
            ydst = [y_own[128 * j:128 * (j + 1), :] for j in range(8)] + [y_samp[:, :]]
            def ln2_sa(t, R):
                rt = resid[t]
                sm, smres = stats_rstd(rt, R, ("r", t))
                ACT(rt[:R], rt[:R], AF.Identity, [("r", t), smres], [("r", t)], scale=sm[:R, 0:1], bias=sm[:R, 1:2])
            def ln2_sb(t, R):
                rt = resid[t]
                TT(rt[:R], rt[:R], lnA[:R], ALU.mult, [("r", t), "lnA"], [("r", t)])
                TT(rt[:R], rt[:R], lnB[:R], ALU.add, [("r", t), "lnB"], [("r", t)], eng="pool")
                DMA("sp", ydst[t], rt[:R], ("yo", t), reads=[("r", t)])

            def samp_state():
                for j in range(16):
                    sl = j % 2
                    DMA("sp", Sin3[sl], sh[j].rearrange("h k v -> k h v"), ("sin3", sl), writes=[("Sin3", sl)])
                    TS(km3[:64], ks_all[:64, :, :], sel[:64, j:j + 1], ALU.mult, ["ks_all", "cst"], ["km3"])
                    def g4(e):
                        ins = None
                        for h in range(NH):
                            ins = e.matmul(ps[:, 6 + h // 4, (h % 4) * 128:(h % 4 + 1) * 128], lhsT=km3[:64, h, :],
                                           rhs=vs_all[:64, h * 128:(h + 1) * 128], start=True, stop=True)
                        return ins
                    PE(g4, ["km3", "vs_all"], [PSB(6), PSB(7)])
                    TT(Sin3[sl], Sin3[sl], C2s[:, :, j:j + 1].broadcast_to([128, NH, 128]), ALU.mult, [("Sin3", sl), "C2s"], [("Sin3", sl)])
                    TT(Sin3[sl], Sin3[sl], ps[:, 6:8, :].rearrange("p b (h v) -> p (b h) v", v=128), ALU.add,
                       [("Sin3", sl), PSB(6), PSB(7)], [("Sin3", sl)])
                    DMA("sp", s_s[j].rearrange("h k v -> k h v"), Sin3[sl], ("sout3", sl), reads=[("Sin3", sl)])
                    yield
                    yield
            bg.append(samp_state())
            gu = [0]; dn = [0]
            for g in range(11):
                for u in range(2):
                    (gw, gres), (uw, ures) = ws.acquire(2)
                    for bb in range(2):
                        blk = 2 * u + bb
                        for (c0, c1) in H_GROUPS:
                            n = c1 - c0
                            gb, ub = (0, 1) if gu[0] % 2 == 0 else (2, 3)
                            sl = gu[0] % 2; gu[0] += 1
                            def f(e, c0=c0, c1=c1, n=n, gb=gb, ub=ub, gw=gw, uw=uw, bb=bb):
                                ins = None
                                for k in range(KC):
                                    ins = e.matmul(ps[:, gb, 0:n], lhsT=gw[:, k, 128 * bb:128 * (bb + 1)], rhs=hT[:, k, c0:c1],
                                                   start=(k == 0), stop=(k == KC - 1))
                                for k in range(KC):
                                    ins = e.matmul(ps[:, ub, 0:n], lhsT=uw[:, k, 128 * bb:128 * (bb + 1)], rhs=hT[:, k, c0:c1],
                                                   start=(k == 0), stop=(k == KC - 1))
                                return ins
                            PE(f, [gres, ures, ("hT", 0), ("hT", 1)], [PSB(gb), PSB(ub)])
                            ACT(sgt[sl][:, 0:n], ps[:, gb, 0:n], AF.Silu, [PSB(gb)], [("sg", sl)])
                            TT(actT[:, blk, c0:c1], sgt[sl][:, 0:n], ps[:, ub, 0:n], ALU.mult, [("sg", sl), PSB(ub)], ["actT"])
                            tick()
                units = ws.acquire(2)
                def down(n, t, c0, R, units=units):
                    b = 4 + dn[0] % 2; dn[0] += 1
                    def f(e):
                        ins = None
                        for kc in range(4):
                            wv = units[kc // 2][0]
                            ins = e.matmul(ps[:R, b, :], lhsT=actT[:, kc, c0:c0 + R], rhs=wv[:, kc % 2, 512 * n:512 * (n + 1)],
                                           start=(kc == 0), stop=(kc == 3))
                        return ins
                    PE(f, [units[0][1], units[1][1], "actT"], [PSB(b)])
                    rc = resid[t][:R, 512 * n:512 * (n + 1)]
                    if g == 0:
                        TT(rc, rc, lnA[:R, 512 * n:512 * (n + 1)], ALU.mult, [("r", t), "lnA"], [("r", t)])
                        TT(rc, rc, lnB[:R, 512 * n:512 * (n + 1)], ALU.add, [("r", t), "lnB"], [("r", t)])
                    TT(rc, rc, ps[:R, b, :], ALU.add, [("r", t), PSB(b)], [("r", t)])
                if g == 0:
                    for n in range(4):
                        for t, (c0, R) in enumerate(OWN_TILES):
                            down(n, t, c0, R)
                    load_ln(ln2_g, ln2_b, 1.0)
                elif g < 10:
                    for n in range(4):
                        for t, (c0, R) in enumerate(OWN_TILES):
                            down(n, t, c0, R)
                else:
                    for t, (c0, R) in enumerate(OWN_TILES):
                        for n in range(4):
                            down(n, t, c0, R)
                        ln2_sa(t, R)
                        if t > 0:
                            ln2_sb(t - 1, OWN_TILES[t - 1][1])
                    ln2_sb(8, OWN_TILES[8][1])
            P.barrier()
        except _Skip:
            P.barrier()
        P.op("sp", lambda e: e.nop(), reads=(), writes=())

        with nc.Block() as block:
            def emit(eng_name, e):
                for (waits, fn, semkey, inc) in P.streams[eng_name]:
                    for (k, v) in waits:
                        e.wait_ge(sems[k], v)
                    ins = fn(e)
                    ins.then_inc(sems[semkey], inc)

            @block.tensor
            def _(e):
                emit("pe", e)

            @block.scalar
            def _(e):
                emit("act", e)

            @block.vector
            def _(e):
                emit("dve", e)

            @block.gpsimd
            def _(e):
                emit("pool", e)

            @block.sync
            def _(e):
                emit("sp", e)
    return nc


_NC_CACHE = {}


def _consts(flagval):
    c = np.zeros((128, NCONST), np.float32)
    c[:, C_ID:C_ID + 128] = np.eye(128, dtype=np.float32)
    s = np.arange(128)[:, None]; t = np.arange(128)[None, :]
    c[:, C_MASK:C_MASK + 128] = (s <= t).astype(np.float32)
    s6 = np.arange(128)[:, None]; t6 = np.arange(64)[None, :]
    c[:, C_SMASK:C_SMASK + 64] = ((s6 // 4 == t6 // 4) & (s6 <= t6)).astype(np.float32)
    c[:, C_SEL:C_SEL + 16] = (np.arange(128)[:, None] // 4 == np.arange(16)[None, :]).astype(np.float32)
    c[:, C_FLAG] = flagval
    c[:, C_ONES:C_ONES + 128] = 1.0
    return c


def kernel(x_prompt, x_sample, state_hgrn, state_conv, meta_tokens, ln0_g, ln0_b,
           w_in, b_f, lb_param, gnorm_g, conv_w, w_o, ln1_g, ln1_b,
           w_gate, w_up, w_down, ln2_g, ln2_b):
    f = lambda a: np.ascontiguousarray(np.asarray(a), dtype=np.float32)
    x_prompt = f(x_prompt); x_sample = f(x_sample); state_hgrn = f(state_hgrn); state_conv = f(state_conv)
    meta = f(meta_tokens)
    shared = {
        "w_in": f(w_in)[0], "w_o": f(w_o)[0], "w_gate": f(w_gate)[0], "w_up": f(w_up)[0], "w_down": f(w_down)[0],
        "ln0_g": f(ln0_g), "ln0_b": f(ln0_b), "ln1_g": f(ln1_g)[0], "ln1_b": f(ln1_b)[0],
        "ln2_g": f(ln2_g)[0], "ln2_b": f(ln2_b)[0], "b_f": f(b_f)[0], "lb_param": f(lb_param),
        "gnorm_g": f(gnorm_g)[0], "conv_w": f(conv_w)[0],
    }
    vecs_host = np.ascontiguousarray(np.concatenate([
        shared["b_f"].reshape(8, 128), shared["gnorm_g"].reshape(8, 128),
        shared["ln0_g"].reshape(16, 128), shared["ln0_b"].reshape(16, 128),
        shared["ln1_g"].reshape(16, 128), shared["ln1_b"].reshape(16, 128),
        shared["conv_w"].reshape(24, 128), shared["lb_param"].reshape(16, 128)], axis=0))
    in_maps = []
    for c in range(8):
        seq, half = c // 2, c % 2
        xo = x_prompt[seq, 1024 * half:1024 * (half + 1)]
        xp = np.zeros((TP, D), np.float32)
        xp[:16] = meta
        if half == 1:
            xp[16:] = x_prompt[seq, 0:1024]
            halo = x_prompt[seq, 1022:1024]
        else:
            halo = meta[14:16]
        xs = np.concatenate([x_sample[16 * c:16 * (c + 1)].reshape(64, D), halo], axis=0)
        m = {k: v for k, v in shared.items() if k not in ("b_f", "lb_param", "gnorm_g", "conv_w")}
        m.update({
            "xo": np.ascontiguousarray(xo), "xs": np.ascontiguousarray(xs), "xp": xp,
            "sh": np.ascontiguousarray(state_hgrn[0, 16 * c:16 * (c + 1)]),
            "sc": np.ascontiguousarray(state_conv[0, 16 * c:16 * (c + 1)].reshape(32, 1024)),
            "consts": _consts(float(half)),
            "vecs": vecs_host,
        })
        in_maps.append(m)
    if "nc" not in _NC_CACHE:
        _NC_CACHE["nc"] = build_nc()
    nc = _NC_CACHE["nc"]
    res = run_bass_kernel_spmd(nc, in_maps, core_ids=list(range(8)))
    R = res.results
    y_prompt = np.zeros((4, 2048, D), np.float32)
    y_sample = np.zeros((128, 4, D), np.float32)
    hp = np.zeros((1, 4, NH, 128, 128), np.float32)
    cp = np.zeros((1, 4, 2, 1024), np.float32)
    hs = np.zeros((1, 128, NH, 128, 128), np.float32)
    cs = np.zeros((1, 128, 2, 1024), np.float32)
    for c in range(8):
        seq, half = c // 2, c % 2
        y_prompt[seq, 1024 * half:1024 * (half + 1)] = R[c]["y_own"]
        y_sample[16 * c:16 * (c + 1)] = R[c]["y_samp"].reshape(16, 4, D)
        hs[0, 16 * c:16 * (c + 1)] = R[c]["s_s"]
        cs[0, 16 * c:16 * (c + 1)] = R[c]["c_s"].reshape(16, 2, 1024)
        if half == 1:
            hp[0, seq] = R[c]["s_p"]
            cp[0, seq] = R[c]["c_p"]
    return (y_prompt, y_sample, hp, cp, hs, cs)
```

```python
import os
import numpy as np
from contextlib import ExitStack
import concourse.bass as bass
import concourse.mybir as mybir
from concourse.bass_utils import run_bass_kernel_spmd

F32 = mybir.dt.float32
BF16 = mybir.dt.bfloat16
AF = mybir.ActivationFunctionType
ALU = mybir.AluOpType

D = 2048
KC = 16
NH = 8
DFF = 5632
NIN = 7168
ALPHA = 2.0 ** 0.25
LN_EPS = 1e-5
RMS_EPS = 1e-6
TO = 1090
OC = 2
SC = 1026
TP = 1040
NSLOT = 3
SLOTW = 4096
OWN_GROUPS = [(0, 512), (512, 1024), (1024, 1090)]
PRE_GROUPS = [(0, 512), (512, 1024), (1024, 1040)]
H_GROUPS = [(2, 514), (514, 1026), (1026, 1090)]
OWN_TILES = [(OC + 128 * j, 128) for j in range(8)] + [(SC, 64)]
PRE_TILES = [(0, 16)] + [(16 + 128 * j, 128) for j in range(8)]

C_ID = 0
C_MASK = 128
C_SMASK = 256
C_SEL = 320
C_FLAG = 336
C_ONES = 337
NCONST = 337 + 128


_STOP = int(os.environ.get("KSTOP", "99"))
_SUB = int(os.environ.get("KSUB", "99"))


class _Skip(Exception):
    pass


class Prog:
    ENG = ("pe", "act", "dve", "pool", "sp")

    def __init__(self):
        self.streams = {e: [] for e in self.ENG}
        self.cnt = {e: 0 for e in self.ENG}
        self.dcnt = {}
        self.waited = {e: {} for e in self.ENG}
        self.res = {}
        self.pending = {e: {} for e in self.ENG}
        self.esem = {}

    def _deps(self, reads, writes):
        need = {}
        def add(t):
            if t is None:
                return
            k, v = t
            if need.get(k, 0) < v:
                need[k] = v
        for r in reads:
            st = self.res.get(r)
            if st:
                add(st["w"])
        for w in writes:
            st = self.res.get(w)
            if st:
                add(st["w"])
                for t in st["r"]:
                    add(t)
        return need

    def op(self, eng, fn, reads=(), writes=(), dma=None):
        def _exp(lst):
            out = []
            for r in lst:
                out.append(r)
                if isinstance(r, tuple) and len(r) == 2 and r[0] == "t":
                    out.append(("t", r[1], 0)); out.append(("t", r[1], 1))
            return out
        reads = _exp(reads); writes = _exp(writes)
        psr = [r for r in reads if isinstance(r, tuple) and r[0] == "ps"]
        if psr:
            reads = [r for r in reads if not (isinstance(r, tuple) and r[0] == "ps")]
            writes = list(writes) + psr
        need = self._deps(reads, writes)
        if dma is not None and self.dcnt.get(dma, 0) > 0:
            if need.get(("d", dma), 0) < self.dcnt[dma]:
                need[("d", dma)] = self.dcnt[dma]
        for k, v in self.pending[eng].items():
            if need.get(k, 0) < v:
                need[k] = v
        self.pending[eng] = {}
        waits = []
        for k, v in need.items():
            if self.waited[eng].get(k, 0) < v:
                self.waited[eng][k] = v
                waits.append((k, v))
        if dma is None:
            self.cnt[eng] += 1
            tk = (("e", eng), self.cnt[eng])
            inc = 1
        else:
            self.dcnt[dma] = self.dcnt.get(dma, 0) + 16
            tk = (("d", dma), self.dcnt[dma])
            inc = 16
        self.streams[eng].append((waits, fn, tk[0], inc))
        for r in reads:
            self.res.setdefault(r, {"w": None, "r": []})["r"].append(tk)
        for w in writes:
            self.res[w] = {"w": tk, "r": []}
        return tk

    def barrier(self):
        allt = {}
        for e in self.ENG:
            if self.cnt[e]:
                allt[("e", e)] = self.cnt[e]
        for d, v in self.dcnt.items():
            if v:
                allt[("d", d)] = v
        for e in self.ENG:
            for k, v in allt.items():
                if self.pending[e].get(k, 0) < v:
                    self.pending[e][k] = v
        self.res = {}


def build_nc():
    nc = bass.Bass("TRN2", target_bir_lowering=False)
    P = Prog()

    def din(name, shape):
        return nc.dram_tensor(name, shape, F32, kind="ExternalInput").ap()

    def dout(name, shape):
        return nc.dram_tensor(name, shape, F32, kind="ExternalOutput").ap()

    xo = din("xo", [1024, D]); xs = din("xs", [66, D]); xp = din("xp", [TP, D])
    sh = din("sh", [16, NH, 128, 128]); scin_d = din("sc", [32, 1024])
    w_in = din("w_in", [D, NIN]); w_o = din("w_o", [D, D])
    w_gate = din("w_gate", [D, DFF]); w_up = din("w_up", [D, DFF]); w_down = din("w_down", [DFF, D])
    ln0_g = din("ln0_g", [D]); ln0_b = din("ln0_b", [D]); ln1_g = din("ln1_g", [D]); ln1_b = din("ln1_b", [D])
    ln2_g = din("ln2_g", [D]); ln2_b = din("ln2_b", [D])
    consts_d = din("consts", [128, NCONST])
    vecs_d = din("vecs", [120, 128])
    y_own = dout("y_own", [1024, D]); y_samp = dout("y_samp", [64, D])
    s_p = dout("s_p", [NH, 128, 128]); s_s = dout("s_s", [16, NH, 128, 128])
    c_p = dout("c_p", [2, 1024]); c_s = dout("c_s", [32, 1024])

    es = ExitStack()
    with es:
        def sb(name, shape, dt=F32):
            return es.enter_context(nc.sbuf_tensor(name, shape, dt))

        arA = sb("arA", [128, KC * TO], BF16)
        arB = sb("arB", [128, KC * TO], BF16)
        NC_W = 17564
        arC = sb("arC", [128, NC_W], F32)
        slots = sb("slots", [128, NSLOT, SLOTW], BF16)
        misc = sb("misc", [128, 5120], F32)
        rsamp = sb("rsamp", [128, D], F32)
        Sf = sb("Sf", [128, NH, 128], F32)
        Sb = sb("Sb", [128, NH, 128], BF16)
        cst = sb("cst", [128, NCONST], F32)
        identb = sb("identb", [128, 128], BF16)
        mask_own = sb("mask_own", [128, TO], BF16)
        mask_pre = sb("mask_pre", [128, TP], BF16)
        cols = sb("cols", [128, 144], F32)
        vs_all = sb("vs_all", [128, 1024], BF16)
        lp = sb("lp", [128, 2, NH], F32)
        C1o = sb("C1o", [128, NH, 8], F32); C2o = sb("C2o", [128, NH, 8], F32)
        C1p = sb("C1p", [128, NH, 9], F32); C2p = sb("C2p", [128, NH, 9], F32)
        C2s = sb("C2s", [128, NH, 16], F32)
        mvs = sb("mvs", [128, 9, 2], F32)
        stt_ = [sb("stt%d" % i, [128, 4, 6], F32) for i in range(4)]
        mv_ = [sb("mv%d" % i, [128, 2], F32) for i in range(4)]
        sm_ = [sb("sm%d" % i, [128, 4], F32) for i in range(4)]
        ulast = sb("ulast", [128, 8, 2], F32)
        ps = es.enter_context(nc.psum_tensor("ps", [128, 8, 512], F32))
        psb = ps[:].bitcast(BF16)

        xnT_pre = arA[:, 0:KC * TP].rearrange("p (k t) -> p k t", k=KC)
        aT = arA[:].rearrange("p (k t) -> p k t", k=KC)
        xnT = arB[:].rearrange("p (k t) -> p k t", k=KC)
        hT = xnT

        def cview(off_w, nbf16, dt=BF16):
            a = arC[:, off_w:off_w + nbf16 // 2]
            return a.bitcast(BF16)

        ka_pre = cview(0, NH * TP).rearrange("p (h t) -> p h t", h=NH)
        v_pre = cview(4160, 9 * 1024).rearrange("p (j c) -> p j c", j=9)
        qa = cview(0, 4 * TO).rearrange("p (h t) -> p h t", h=4)
        qb = cview(2180, 4 * TO).rearrange("p (h t) -> p h t", h=4)
        ka = cview(4360, 4 * TO).rearrange("p (h t) -> p h t", h=4)
        v_tok = cview(6540, 9 * 512).rearrange("p (j c) -> p j c", j=9)
        DOFF = 8844
        xst = [arC[:, DOFF:DOFF + 2048], arC[:, DOFF + 2048:DOFF + 4096],
               arC[:, DOFF + 4360:DOFF + 4360 + 2048], arC[:, DOFF + 4360 + 2048:DOFF + 4360 + 4096]]
        g2 = arC[:, DOFF:DOFF + 4 * TO].rearrange("p (h t) -> p h t", h=4)
        EOFF = DOFF + 4360
        tt = [arC[:, EOFF + i * TO:EOFF + (i + 1) * TO] for i in range(4)]
        katok = tt[0][:, 0:512].bitcast(BF16).rearrange("p (h k) -> p h k", h=NH)
        ATm = tt[0][:, 512:768].bitcast(BF16).rearrange("p (h t) -> p h t", h=4)
        km = tt[0][:, 768:1024].bitcast(BF16).rearrange("p (h k) -> p h k", h=4)
        sq = tt[1][:, 0:512]
        on = tt[1][:, 512:1024]
        rs = tt[2][:, 0:512]
        tmpS = tt[3][:, 0:1024].rearrange("p (h v) -> p h v", h=NH)
        resid = [arC[:, 2048 * j:2048 * (j + 1)] for j in range(8)] + [rsamp[:]]
        Sin = [misc[:, 512 * i:512 * (i + 1)].rearrange("p (h v) -> p h v", h=4) for i in range(4)]
        Sjb = [misc[:, 2048 + 256 * i:2048 + 256 * (i + 1)].bitcast(BF16).rearrange("p (h v) -> p h v", h=4) for i in range(4)]
        lnA = misc[:, 0:2048]; lnB = misc[:, 2048:4096]
        sgt = [misc[:, 4096:4608], misc[:, 4608:5120]]
        actT = arA[:, 0:4 * TO].rearrange("p (k t) -> p k t", k=4)
        XS3 = 4 * TO + 2 * SLOTW
        Sin3 = [arA[:, XS3 + 2048 * i:XS3 + 2048 * (i + 1)].bitcast(F32).rearrange("p (h v) -> p h v", h=NH) for i in range(2)]
        km3 = Sf[:].rearrange("p h v -> p (h v)")[:, 0:512].bitcast(BF16).rearrange("p (h k) -> p h k", h=NH)

        ks_all = mask_pre[:, 0:1024].rearrange("p (h k) -> p h k", h=NH)
        vecs = misc[:, 3584:3712]
        gsq = misc[:, 3584:4096]
        gon = misc[:, 4096:4608]
        scT = misc[:, 4608:4864].rearrange("p (j s) -> p j s", j=8)
        ulast_s = misc[:, 4864:5120].rearrange("p (j s) -> p j s", j=8)
        grs = rsamp[:, 0:512]
        gtmp = rsamp[:, 512:1024].rearrange("p (h v) -> p h v", h=4)
        gkatok = [rsamp[:, 1024:1280].bitcast(BF16).rearrange("p (h k) -> p h k", h=4),
                  rsamp[:, 1280:1536].bitcast(BF16).rearrange("p (h k) -> p h k", h=4)]
        gATm = [rsamp[:, 1536:1792].bitcast(BF16).rearrange("p (h k) -> p h k", h=4),
                rsamp[:, 1792:2048].bitcast(BF16).rearrange("p (h k) -> p h k", h=4)]
        scin = rsamp[:32, 0:1024]
        cout = rsamp[:32, 1024:2048]
        cout2 = misc[:2, 2560:3584]
        ident = cst[:, C_ID:C_ID + 128]
        maskT = cst[:, C_MASK:C_MASK + 128]
        smask = cst[:, C_SMASK:C_SMASK + 64]
        sel = cst[:, C_SEL:C_SEL + 16]
        flag = cst[:, C_FLAG:C_FLAG + 1]
        ones = cst[:, C_ONES:C_ONES + 128]
        bfneg = cols[:, 0:8]; oml = cols[:, 8:16]; omlf = cols[:, 16:24]; gn = cols[:, 24:32]
        cw = cols[:, 32:56].rearrange("p (j i) -> p j i", i=3)
        g0c = cols[:, 64:80]; b0c = cols[:, 80:96]; g1c = cols[:, 96:112]; b1c = cols[:, 112:128]
        bfr = cols[:, 128:136]
        noml = cols[:, 56:64]; nomlf = cols[:, 136:144]

        sems = {}
        def getsem(key):
            if key not in sems:
                sems[key] = es.enter_context(nc.semaphore("s%d" % len(sems)))
            return sems[key]
        for e in Prog.ENG:
            getsem(("e", e))

        def DMA(eng, out, in_, sem, reads=(), writes=(), slow=False):
            def fn(e):
                if slow:
                    return e.dma_start(out=out, in_=in_, allow_slow_non_contiguous=True)
                return e.dma_start(out=out, in_=in_)
            getsem(("d", sem))
            return P.op(eng, fn, reads=reads, writes=writes, dma=sem)

        def ACT(out, in_, func, reads, writes, scale=1.0, bias=0.0):
            P.op("act", lambda e: e.activation(out=out, in_=in_, func=func, bias=bias, scale=scale),
                 reads=reads, writes=writes)

        def TT(out, in0, in1, op, reads, writes, eng="dve"):
            P.op(eng, lambda e: e.tensor_tensor(out=out, in0=in0, in1=in1, op=op), reads=reads, writes=writes)

        def TS(out, in0, s1, op0, reads, writes, s2=None, op1=None, eng="dve"):
            if op1 is None:
                P.op(eng, lambda e: e.tensor_scalar(out=out, in0=in0, scalar1=s1, scalar2=None, op0=op0),
                     reads=reads, writes=writes)
            else:
                P.op(eng, lambda e: e.tensor_scalar(out=out, in0=in0, scalar1=s1, scalar2=s2, op0=op0, op1=op1),
                     reads=reads, writes=writes)

        def STT(out, in0, scalar, in1, op0, op1, reads, writes):
            P.op("dve", lambda e: e.scalar_tensor_tensor(out=out, in0=in0, scalar=scalar, in1=in1, op0=op0, op1=op1),
                 reads=reads, writes=writes)

        def PE(fn, reads, writes):
            P.op("pe", fn, reads=reads, writes=writes)

        def PSB(b):
            return ("ps", b)

        class WS:
            def __init__(self):
                self.units = []
                self.issued = 0
                self.next_idx = 0
                self.pool = []
                self.slot_of = {}
            def add(self, ap, kind, ncols):
                self.units.append((ap, kind, ncols))
            def grow(self, new_slots):
                self.pool = list(new_slots) + self.pool
            def _view(self, flat, kind, ncols):
                if kind == "col":
                    return flat[:, 0:KC * ncols].rearrange("p (k c) -> p k c", k=KC)
                return flat[:, :].rearrange("p (k c) -> p k c", k=2)
            def _issue(self, u):
                ap, kind, ncols = self.units[u]
                flat, res = self.pool.pop(0)
                self.pool.append((flat, res))
                self.slot_of[u] = (flat, res)
                DMA("pool", self._view(flat, kind, ncols), ap.rearrange("(k p) c -> p k c", p=128), res, writes=[res])
            def acquire(self, n):
                i0 = self.next_idx
                upto = min(len(self.units), i0 + len(self.pool))
                while self.issued < upto:
                    self._issue(self.issued)
                    self.issued += 1
                self.next_idx += n
                out = []
                for u in range(i0, i0 + n):
                    ap, kind, ncols = self.units[u]
                    flat, res = self.slot_of[u]
                    out.append((self._view(flat, kind, ncols), res))
                return out

        ws = WS()
        ws.grow([(slots[:, i, :], ("slot", i)) for i in range(NSLOT)])
        for u in range(4):
            ws.add(w_in[:, 1024 + 256 * u:1024 + 256 * (u + 1)], "col", 256)
        for u in range(4):
            ws.add(w_in[:, 2048 + 256 * u:2048 + 256 * (u + 1)], "col", 256)
        for H in range(2):
            for i in range(4):
                h = 4 * H + i
                ws.add(w_in[:, 1024 + 128 * h:1024 + 128 * (h + 1)], "col", 128)
                ws.add(w_in[:, 128 * h:128 * (h + 1)], "col", 128)
            for u in range(2):
                ws.add(w_in[:, 2048 + 512 * H + 256 * u:2048 + 512 * H + 256 * (u + 1)], "col", 256)
            for u in range(2):
                ws.add(w_in[:, 3072 + 512 * H + 256 * u:3072 + 512 * H + 256 * (u + 1)], "col", 256)
            for j in range(4 * H, 4 * H + 4):
                ws.add(w_in[:, 5120 + 128 * j:5120 + 128 * (j + 1)], "col", 128)
                ws.add(w_in[:, 6144 + 128 * j:6144 + 128 * (j + 1)], "col", 128)
                ws.add(w_in[:, 4096 + 128 * j:4096 + 128 * (j + 1)], "col", 128)
        for n in range(8):
            ws.add(w_o[:, 256 * n:256 * (n + 1)], "col", 256)
        for g in range(11):
            for u in range(2):
                ws.add(w_gate[:, 512 * g + 256 * u:512 * g + 256 * (u + 1)], "col", 256)
                ws.add(w_up[:, 512 * g + 256 * u:512 * g + 256 * (u + 1)], "col", 256)
            for u in range(2):
                ws.add(w_down[512 * g + 256 * u:512 * g + 256 * (u + 1), :], "row", 2048)

        bg = []
        def tick(n=1):
            for _ in range(n):
                for g_ in list(bg):
                    try:
                        next(g_)
                    except StopIteration:
                        bg.remove(g_)
        def drain():
            while bg:
                tick()

        try:
            if _STOP < -2:
                raise _Skip()
            nsm = [0]
            def small_load(out, in_, writes, slow=True):
                nsm[0] += 1
                DMA("act", out, in_, ("c", nsm[0]), writes=writes, slow=slow)

            small_load(cst[:], consts_d, ["cst"], slow=False)
            r_bf, r_gn, r_g0, r_b0, r_g1, r_b1, r_cw, r_lp = 0, 8, 16, 32, 48, 64, 80, 104
            vrow = [120]
            DMA("sp", vecs[0:120, :], vecs_d, ("c", 99), writes=["vecs"])
            NV = vrow[0]
            PE(lambda e: e.transpose(out=ps[:, 1, 0:NV], in_=vecs[0:NV, :], identity=ident[:NV, :NV]), ["vecs", "cst"], [PSB(1)])
            def vcopy(dst, r0, n):
                P.op("dve", lambda e: e.tensor_copy(out=dst, in_=ps[:, 1, r0:r0 + n]), reads=[PSB(1)], writes=["cols"])
            vcopy(bfr, r_bf, 8); vcopy(gn, r_gn, 8); vcopy(g0c, r_g0, 16); vcopy(b0c, r_b0, 16)
            vcopy(g1c, r_g1, 16); vcopy(b1c, r_b1, 16)
            for i in range(3):
                P.op("dve", lambda e, i=i: e.tensor_copy(out=cw[:, :, i], in_=ps[:, 1, r_cw + 8 * i:r_cw + 8 * i + 8]), reads=[PSB(1)], writes=["cols"])
            P.op("dve", lambda e: e.tensor_copy(out=lp[:].rearrange("p r h -> p (r h)"), in_=ps[:, 1, r_lp:r_lp + 16]), reads=[PSB(1)], writes=["lp"])
            small_load(scin, scin_d, ["scin"], slow=False)

            if _STOP < -1:
                raise _Skip()
            P.op("dve", lambda e: e.tensor_copy(out=identb[:], in_=ident), reads=["cst"], writes=["identb"])
            TS(bfneg, bfr, -1.0, ALU.mult, ["cols"], ["cols"])
            TT(oml, lp[:, 1, :], lp[:, 0, :], ALU.subtract, ["lp", "cols"], ["cols"])
            ACT(oml, oml, AF.Sigmoid, ["cols"], ["cols"])
            TS(omlf, oml, flag, ALU.mult, ["cols", "cst"], ["cols"])
            TS(noml, oml, -1.0, ALU.mult, ["cols"], ["cols"])
            TS(nomlf, omlf, -1.0, ALU.mult, ["cols"], ["cols"])
            P.op("pool", lambda e: e.memset(mask_own[:], 1.0), writes=["mask_own"])
            P.op("pool", lambda e: e.memset(mask_own[:, OC:OC + 1024].rearrange("p (c t) -> p c t", t=128)[:, :, 0:1], 0.0),
                 writes=["mask_own"])
            P.op("pool", lambda e: e.memset(mask_own[:, SC:SC + 64].rearrange("p (c t) -> p c t", t=4)[:, :, 0:1], 0.0),
                 writes=["mask_own"])
            P.op("pool", lambda e: e.memset(mask_own[:, 0:1], 0.0), writes=["mask_own"])
            P.op("pool", lambda e: e.memset(mask_pre[:], 1.0), writes=["mask_pre"])
            P.op("pool", lambda e: e.memset(mask_pre[:, 16:TP].rearrange("p (c t) -> p c t", t=128)[:, :, 0:1], 0.0),
                 writes=["mask_pre"])
            P.op("pool", lambda e: e.memset(mask_pre[:, 0:1], 0.0), writes=["mask_pre"])
            P.op("dve", lambda e: e.memset(Sf[:], 0.0), writes=["Sf"])
            def f_sct(e):
                ins = None
                for j in range(8):
                    ins = e.transpose(out=ps[:, 0, j * 32:(j + 1) * 32], in_=scin[:, j * 128:(j + 1) * 128], identity=ident[:32, :32])
                return ins
            PE(f_sct, ["scin", "cst"], [PSB(0)])
            ACT(scT.rearrange("p j s -> p (j s)"), ps[:, 0, 0:256], AF.Copy, [PSB(0)], ["scT"])

            if _STOP < 0:
                raise _Skip()
            stp = [0]
            def stats_rstd(src, R, res):
                q = stp[0] % 4; stp[0] += 1
                stt, mv, sm = stt_[q], mv_[q], sm_[q]
                def f(e):
                    ins = None
                    for i in range(4):
                        ins = e.bn_stats(out=stt[:R, i, :], in_=src[:R, i * 512:(i + 1) * 512])
                    return ins
                P.op("dve", f, reads=[res], writes=[("stt", q)])
                P.op("dve", lambda e: e.bn_aggr(out=mv[:R, :], in_=stt[:R].rearrange("p a b -> p (a b)")),
                     reads=[("stt", q)], writes=[("mv", q)])
                ACT(sm[:R, 2:3], mv[:R, 1:2], AF.Ln, [("mv", q)], [("sm", q)], bias=LN_EPS)
                ACT(sm[:R, 0:1], sm[:R, 2:3], AF.Exp, [("sm", q)], [("sm", q)], scale=-0.5)
                STT(sm[:R, 1:2], mv[:R, 0:1], -1.0, sm[:R, 0:1], ALU.mult, ALU.mult, [("mv", q), ("sm", q)], [("sm", q)])
                return sm, ("sm", q)

            tile_ctr = [0]
            nxs = [4]
            def ln0_tile(src_ap, R, dstT, xres, pieces, gc, bc, save_idx=None, bankset=None):
                ti = tile_ctr[0]; tile_ctr[0] += 1
                sl = ti % nxs[0]
                xt = xst[sl]
                alias = {2: [("t", 0), ("t", 1)], 3: [("t", 2), ("t", 3)]}.get(sl, [])
                DMA("sp", xt[:R], src_ap, ("xst", sl), writes=[("xst", sl)] + alias)
                sm, smres = stats_rstd(xt, R, ("xst", sl))
                if save_idx is not None:
                    P.op("dve", lambda e: e.tensor_copy(out=mvs[:R, save_idx, :], in_=sm[:R, 0:2]), reads=[smres], writes=["mvs"])
                TS(xt[:R], xt[:R], sm[:R, 0:1], ALU.mult, [("xst", sl), smres], [("xst", sl)], s2=sm[:R, 1:2], op1=ALU.add, eng="pool")
                yield
                b0 = 4 * (ti % 2) if bankset is None else bankset
                def f(e):
                    ins = None
                    for kb in range(KC):
                        ins = e.transpose(out=ps[:, b0 + kb // 4, (kb % 4) * 128:(kb % 4) * 128 + R],
                                          in_=xt[:R, kb * 128:(kb + 1) * 128], identity=ident[:R, :R])
                    return ins
                PE(f, [("xst", sl), "cst"], [PSB(b0 + i) for i in range(4)])
                yield
                for kb in range(KC):
                    for (r0, nr, c0) in pieces:
                        src = ps[:, b0 + kb // 4, (kb % 4) * 128 + r0:(kb % 4) * 128 + r0 + nr]
                        dst = dstT[:, kb, c0:c0 + nr]
                        if (kb // 4) % 2 == 0:
                            ACT(dst, src, AF.Identity, [PSB(b0 + kb // 4), "cols"], [(xres, 0)], scale=gc[:, kb:kb + 1], bias=bc[:, kb:kb + 1])
                        else:
                            TS(dst, src, gc[:, kb:kb + 1], ALU.mult, [PSB(b0 + kb // 4), "cols"], [(xres, 1)],
                               s2=bc[:, kb:kb + 1], op1=ALU.add)

            def run_gen(g_):
                for _ in g_:
                    pass

            pg = [ln0_tile(xp[c0:c0 + R, :], R, xnT_pre, "xTp", [(0, R, c0)], g0c, b0c) for (c0, R) in PRE_TILES]
            for s_ in range(len(pg) + 2):
                if s_ < len(pg):
                    next(pg[s_])
                if 0 <= s_ - 1 < len(pg):
                    next(pg[s_ - 1])
                if 0 <= s_ - 2 < len(pg):
                    run_gen(pg[s_ - 2])

            def p0_own():
                tile_ctr[0] = 12
                gens = [ln0_tile(xo[128 * j:128 * (j + 1), :], 128, xnT, "xT", [(0, 128, OC + 128 * j)], g0c, b0c,
                                 save_idx=j, bankset=4) for j in range(8)]
                gens.append(ln0_tile(xs[:, :], 66, xnT, "xT", [(0, 64, SC), (64, 2, 0)], g0c, b0c, save_idx=8, bankset=4))
                next(gens[0]); next(gens[1])
                yield
                for j in range(9):
                    run_gen(gens[j])
                    if j + 2 < 9:
                        next(gens[j + 2])
                    yield
            bg.append(p0_own())

            if _STOP < 1:
                drain()
                raise _Skip()
            blk_ctr = [0]
            fm_mode = ["low"]
            def fm_block(wv, wres, cb, xT, xres, groups):
                bi = blk_ctr[0]; blk_ctr[0] += 1
                if fm_mode[0] == "dbl":
                    banks = [3 * (bi % 2) + i for i in range(3)]
                else:
                    banks = [0, 1, 2]
                def f(e):
                    ins = None
                    for k in range(KC):
                        for gi, (c0, c1) in enumerate(groups):
                            ins = e.matmul(ps[:, banks[gi], 0:c1 - c0], lhsT=wv[:, k, cb:cb + 128], rhs=xT[:, k, c0:c1],
                                           start=(k == 0), stop=(k == KC - 1))
                    return ins
                PE(f, [wres, (xres, 0), (xres, 1)], [PSB(b) for b in banks])
                return banks

            def per_group(groups, banks, fn):
                for gi, (c0, c1) in enumerate(groups):
                    fn(ps[:, banks[gi], 0:c1 - c0], c0, c1, PSB(banks[gi]))

            vt_ctr = [0]
            def v_tile(units, xT, xres, c0, R, vdst, ti, col0, banks=(6, 7)):
                b = banks[vt_ctr[0] % len(banks)]; vt_ctr[0] += 1
                def f(e):
                    ins = None
                    for u, (wv, wres) in enumerate(units):
                        for k in range(KC):
                            ins = e.matmul(ps[:R, b, 256 * u:256 * (u + 1)], lhsT=xT[:, k, c0:c0 + R], rhs=wv[:, k, 0:256],
                                           start=(k == 0), stop=(k == KC - 1))
                    return ins
                PE(f, [units[0][1], units[1][1], (xres, 0), (xres, 1)], [PSB(b)])
                ACT(vdst[:R, ti, col0:col0 + 512], ps[:R, b, 0:512], AF.Copy, [PSB(b)], ["v"])

            fm_mode[0] = "dbl"
            PH = [(0, 528), (528, TP)]
            for u in range(4):
                (wv, wres), = ws.acquire(1)
                for hh in range(2):
                    h = 2 * u + hh
                    if h == 6:
                        tick()
                    banks = fm_block(wv, wres, 128 * hh, xnT_pre, "xTp", PRE_GROUPS)
                    T = lambda k, hf: ("t", k, hf)
                    sg = lambda src, c0, c1, pres, hf: ACT(tt[0][:, c0:c1], src, AF.Sigmoid, [pres, "cols"], [T(0, hf)],
                                                           scale=-1.0, bias=bfneg[:, h:h + 1])
                    sg(ps[:, banks[0], 0:512], 0, 512, PSB(banks[0]), 0)
                    sg(ps[:, banks[1], 0:16], 512, 528, PSB(banks[1]), 0)
                    sg(ps[:, banks[1], 16:512], 528, 1024, PSB(banks[1]), 1)
                    sg(ps[:, banks[2], 0:16], 1024, 1040, PSB(banks[2]), 1)
                    ACT(tt[1][:, 0:16], tt[0][:, 0:16], AF.Ln, [T(0, 0), "cols"], [T(1, 0)], scale=noml[:, h:h + 1], bias=1.0)
                    ACT(tt[1][:, 16:528], tt[0][:, 16:528], AF.Ln, [T(0, 0), "cols"], [T(1, 0)], scale=nomlf[:, h:h + 1], bias=1.0)
                    ACT(tt[1][:, 528:TP], tt[0][:, 528:TP], AF.Ln, [T(0, 1), "cols"], [T(1, 1)], scale=nomlf[:, h:h + 1], bias=1.0)
                    for hf, (a0, a1) in enumerate(PH):
                        P.op("dve", lambda e, a0=a0, a1=a1: e.tensor_tensor_scan(
                            out=tt[2][:, a0:a1], data0=mask_pre[:, a0:a1], data1=tt[1][:, a0:a1], initial=0.0, op0=ALU.mult, op1=ALU.add),
                            reads=[T(1, hf), "mask_pre"], writes=[T(2, hf)])
                    TT(tt[1][:, 0:16], tt[2][:, 0:16], tt[2][:, 8:9].broadcast_to([128, 16]), ALU.subtract, [T(2, 0)], [T(1, 0)])
                    b3 = [tt[2][:, 16:528].rearrange("p (c t) -> p c t", t=128), tt[2][:, 528:TP].rearrange("p (c t) -> p c t", t=128)]
                    d3 = [tt[1][:, 16:528].rearrange("p (c t) -> p c t", t=128), tt[1][:, 528:TP].rearrange("p (c t) -> p c t", t=128)]
                    for hf in range(2):
                        TT(d3[hf], b3[hf], b3[hf][:, :, 64:65].broadcast_to([128, 4, 128]), ALU.subtract, [T(2, hf)], [T(1, hf)])
                    for hf, (a0, a1) in enumerate(PH):
                        ACT(tt[3][:, a0:a1], tt[1][:, a0:a1], AF.Exp, [T(1, hf)], [T(3, hf)], scale=-1.0)
                    STT(ka_pre[:, h, 0:16], tt[0][:, 0:16], oml[:, h:h + 1], tt[3][:, 0:16], ALU.mult, ALU.mult,
                        [T(0, 0), T(3, 0), "cols"], [("ka", h)])
                    STT(ka_pre[:, h, 16:528], tt[0][:, 16:528], omlf[:, h:h + 1], tt[3][:, 16:528], ALU.mult, ALU.mult,
                        [T(0, 0), T(3, 0), "cols"], [("ka", h)])
                    STT(ka_pre[:, h, 528:TP], tt[0][:, 528:TP], omlf[:, h:h + 1], tt[3][:, 528:TP], ALU.mult, ALU.mult,
                        [T(0, 1), T(3, 1), "cols"], [("ka", h)])
                    ACT(C2p[:, h, 0:1], tt[2][:, 15:16], AF.Exp, [T(2, 0)], ["C2p"])
                    ACT(C1p[:, h, 0:1], tt[1][:, 15:16], AF.Exp, [T(1, 0)], ["C1p"])
                    for hf in range(2):
                        ACT(C2p[:, h, 1 + 4 * hf:5 + 4 * hf], b3[hf][:, :, 127], AF.Exp, [T(2, hf)], ["C2p"])
                        ACT(C1p[:, h, 1 + 4 * hf:5 + 4 * hf], d3[hf][:, :, 127], AF.Exp, [T(1, hf)], ["C1p"])

            def scan_tr(c, c0, R):
                xb = 3
                def f(e):
                    ins = None
                    for h in range(NH):
                        ins = e.transpose(out=psb[:R, xb, h * 128:(h + 1) * 128], in_=ka_pre[:, h, c0:c0 + R], identity=identb[:, :])
                    return ins
                PE(f, [("ka", h) for h in range(NH)] + ["identb"], [PSB(xb)])
                ACT(katok[:R].rearrange("p h k -> p (h k)"), psb[:R, xb, 0:1024], AF.Copy, [PSB(xb)], [("t", 0)])

            def scan_upd(c, c0, R):
                pb = 4 + 2 * (c % 2)
                def f2(e):
                    ins = None
                    for h in range(NH):
                        ins = e.matmul(ps[:, pb + h // 4, (h % 4) * 128:(h % 4 + 1) * 128], lhsT=katok[:R, h, :],
                                       rhs=v_pre[:R, c, h * 128:(h + 1) * 128], start=True, stop=True)
                    return ins
                PE(f2, [("t", 0), "v"], [PSB(pb), PSB(pb + 1)])
                TT(Sf[:], Sf[:], C2p[:, :, c:c + 1].broadcast_to([128, NH, 128]), ALU.mult, ["Sf", "C2p"], ["Sf"])
                pv = ps[:, pb:pb + 2, :].rearrange("p b (h v) -> p (b h) v", v=128)
                TT(tmpS, pv, C1p[:, :, c:c + 1].broadcast_to([128, NH, 128]), ALU.mult, [PSB(pb), PSB(pb + 1), "C1p"], [("t", 3)])
                TT(Sf[:], Sf[:], tmpS, ALU.add, ["Sf", ("t", 3)], ["Sf"])

            units0 = ws.acquire(2)
            for ti, (c0, R) in enumerate(PRE_TILES):
                v_tile(units0, xnT_pre, "xTp", c0, R, v_pre, ti, 0, banks=(0, 1, 2))
                tick()
            drain()
            units1 = ws.acquire(2)
            for ti, (c0, R) in enumerate(PRE_TILES):
                v_tile(units1, xnT_pre, "xTp", c0, R, v_pre, ti, 512, banks=(0, 1, 2))
                if ti > 0:
                    scan_upd(ti - 1, *PRE_TILES[ti - 1])
                scan_tr(ti, c0, R)
            scan_upd(len(PRE_TILES) - 1, *PRE_TILES[-1])
            P.barrier()

            if _STOP < 2:
                raise _Skip()
            fm_mode[0] = "dbl"
            own3 = lambda t: t[:, OC:OC + 1024].rearrange("p (c t) -> p c t", t=128)
            smp3 = lambda t: t[:, SC:SC + 64].rearrange("p (c t) -> p c t", t=4)

            HALVES = [(0, 514), (514, TO)]
            def evac_split(banks, fn):
                fn(ps[:, banks[0], 0:512], 0, 512, PSB(banks[0]), 0)
                fn(ps[:, banks[1], 0:2], 512, 514, PSB(banks[1]), 0)
                fn(ps[:, banks[1], 2:512], 514, 1024, PSB(banks[1]), 1)
                fn(ps[:, banks[2], 0:66], 1024, 1090, PSB(banks[2]), 1)

            def head_elementwise(H, i):
                h = 4 * H + i
                T = lambda k, hf: ("t", k, hf)
                HV = list(enumerate(HALVES))
                (wv, wres), = ws.acquire(1)
                banks = fm_block(wv, wres, 0, xnT, "xT", OWN_GROUPS)
                evac_split(banks, lambda src, c0, c1, pres, hf: ACT(
                    tt[0][:, c0:c1], src, AF.Sigmoid, [pres, "cols"], [T(0, hf)], scale=-1.0, bias=bfneg[:, h:h + 1]))
                for hf, (a0, a1) in HV:
                    ACT(tt[1][:, a0:a1], tt[0][:, a0:a1], AF.Ln, [T(0, hf), "cols"], [T(1, hf)], scale=noml[:, h:h + 1], bias=1.0)
                for hf, (a0, a1) in HV:
                    P.op("dve", lambda e, a0=a0, a1=a1: e.tensor_tensor_scan(
                        out=tt[2][:, a0:a1], data0=mask_own[:, a0:a1], data1=tt[1][:, a0:a1], initial=0.0, op0=ALU.mult, op1=ALU.add),
                        reads=[T(1, hf), "mask_own"], writes=[T(2, hf)])
                bo0 = tt[2][:, OC:OC + 512].rearrange("p (c t) -> p c t", t=128)
                do0 = tt[1][:, OC:OC + 512].rearrange("p (c t) -> p c t", t=128)
                bo1 = tt[2][:, OC + 512:OC + 1024].rearrange("p (c t) -> p c t", t=128)
                do1 = tt[1][:, OC + 512:OC + 1024].rearrange("p (c t) -> p c t", t=128)
                bs = smp3(tt[2]); dsm = smp3(tt[1])
                TT(do0, bo0, bo0[:, :, 64:65].broadcast_to([128, 4, 128]), ALU.subtract, [T(2, 0)], [T(1, 0)])
                P.op("dve", lambda e: e.memset(tt[1][:, 0:2], 0.0), reads=[T(2, 0)], writes=[T(1, 0)])
                TT(do1, bo1, bo1[:, :, 64:65].broadcast_to([128, 4, 128]), ALU.subtract, [T(2, 1)], [T(1, 1)])
                TT(dsm, bs, bs[:, :, 3:4].broadcast_to([128, 16, 4]), ALU.subtract, [T(2, 1)], [T(1, 1)])
                for hf, (a0, a1) in HV:
                    ACT(tt[3][:, a0:a1], tt[1][:, a0:a1], AF.Exp, [T(1, hf)], [T(3, hf)], scale=-1.0)
                for hf, (a0, a1) in HV:
                    STT(ka[:, i, a0:a1], tt[0][:, a0:a1], oml[:, h:h + 1], tt[3][:, a0:a1], ALU.mult, ALU.mult,
                        [T(0, hf), T(3, hf), "cols"], [("ka", i)])
                ACT(C2o[:, h, 0:4], bo0[:, :, 127], AF.Exp, [T(2, 0)], ["C2o"])
                ACT(C1o[:, h, 0:4], do0[:, :, 127], AF.Exp, [T(1, 0)], ["C1o"])
                ACT(C2o[:, h, 4:8], bo1[:, :, 127], AF.Exp, [T(2, 1)], ["C2o"])
                ACT(C1o[:, h, 4:8], do1[:, :, 127], AF.Exp, [T(1, 1)], ["C1o"])
                ACT(C2s[:, h, :], bs[:, :, 3], AF.Exp, [T(2, 1)], ["C2s"])
                (wv, wres), = ws.acquire(1)
                banks = fm_block(wv, wres, 0, xnT, "xT", OWN_GROUPS)
                evac_split(banks, lambda src, c0, c1, pres, hf: ACT(
                    tt[0][:, c0:c1], src, AF.Silu, [pres], [T(0, hf)]))
                for hf, (a0, a1) in HV:
                    ACT(tt[3][:, a0:a1], tt[1][:, a0:a1], AF.Exp, [T(1, hf)], [T(3, hf)])
                for hf, (a0, a1) in HV:
                    TT(qa[:, i, a0:a1], tt[0][:, a0:a1], tt[3][:, a0:a1], ALU.mult, [T(0, hf), T(3, hf)], [("qa", i)])
                for hf, (a0, a1) in HV:
                    ACT(tt[3][:, a0:a1], tt[2][:, a0:a1], AF.Exp, [T(2, hf)], [T(3, hf)])
                for hf, (a0, a1) in HV:
                    TT(qb[:, i, a0:a1], tt[0][:, a0:a1], tt[3][:, a0:a1], ALU.mult, [T(0, hf), T(3, hf)], [("qb", i)])

            def gla_half(H):
                hs = slice(4 * H, 4 * H + 4)
                XB, YB, VB = 3, 4, 7
                P.op("act", lambda e: e.activation(out=Sb[:, hs, :], in_=Sf[:, hs, :], func=AF.Copy), reads=["Sf"], writes=["Sb"])

                def pieceA(c):
                    cc = OC + 128 * c
                    q = c % 2
                    def f1(e):
                        ins = None
                        for i in range(4):
                            ins = e.transpose(out=psb[:, XB, i * 128:(i + 1) * 128], in_=ka[:, i, cc:cc + 128], identity=identb[:, :])
                        for i in range(4):
                            ins = e.matmul(ps[:, YB, i * 128:(i + 1) * 128], lhsT=ka[:, i, cc:cc + 128], rhs=qa[:, i, cc:cc + 128],
                                           start=True, stop=True)
                        return ins
                    PE(f1, [("ka", i) for i in range(4)] + [("qa", i) for i in range(4)] + ["identb"], [PSB(XB), PSB(YB)])
                    ACT(gkatok[q][:, :, :].rearrange("p h k -> p (h k)"), psb[:, XB, 0:512], AF.Copy, [PSB(XB)], [("gk", q)])
                    TT(gATm[q][:, :, :], ps[:, YB, :].rearrange("p (h t) -> p h t", h=4),
                       maskT.rearrange("p (o t) -> p o t", o=1).broadcast_to([128, 4, 128]), ALU.mult, [PSB(YB), "cst"], [("ga", q)])

                def pieceB(c):
                    cc = OC + 128 * c
                    q = c % 2
                    zb = 5 + q
                    def f3(e):
                        ins = None
                        for i in range(4):
                            ins = e.matmul(ps[:, VB, i * 128:(i + 1) * 128], lhsT=gkatok[q][:, i, :], rhs=v_tok[:, c, i * 128:(i + 1) * 128],
                                           start=True, stop=True)
                        return ins
                    PE(f3, [("gk", q), "v"], [PSB(VB)])
                    def f2(e):
                        ins = None
                        for i in range(4):
                            e.matmul(ps[:, zb, i * 128:(i + 1) * 128], lhsT=v_tok[:, c, i * 128:(i + 1) * 128], rhs=gATm[q][:, i, :],
                                     start=True, stop=False)
                            ins = e.matmul(ps[:, zb, i * 128:(i + 1) * 128], lhsT=Sb[:, 4 * H + i, :], rhs=qb[:, i, cc:cc + 128],
                                           start=False, stop=True)
                        return ins
                    PE(f2, [("ga", q), "v", "Sb"] + [("qb", i) for i in range(4)], [PSB(zb)])
                    ACT(gsq[:, 0:512], ps[:, zb, 0:512], AF.Square, [PSB(zb)], ["gsq"])
                    TT(Sf[:, hs, :], Sf[:, hs, :], C2o[:, hs, c:c + 1].broadcast_to([128, 4, 128]), ALU.mult, ["Sf", "C2o"], ["Sf"])
                    TT(gtmp[:, :, :], ps[:, VB, :].rearrange("p (h v) -> p h v", h=4),
                       C1o[:, hs, c:c + 1].broadcast_to([128, 4, 128]), ALU.mult, [PSB(VB), "C1o"], ["gtmp"])
                    TT(Sf[:, hs, :], Sf[:, hs, :], gtmp[:, :, :], ALU.add, ["Sf", "gtmp"], ["Sf"])
                    if c < 7:
                        P.op("act", lambda e: e.activation(out=Sb[:, hs, :], in_=Sf[:, hs, :], func=AF.Copy), reads=["Sf"], writes=["Sb"])

                def pieceC(zb, ncol, dst, gv, hsplit, hw):
                    PE(lambda e: e.matmul(ps[:, YB, 0:ncol], lhsT=ones, rhs=gsq[:, 0:ncol], start=True, stop=True),
                       ["gsq", "cst"], [PSB(YB)])
                    ACT(grs[:, 0:ncol], ps[:, YB, 0:ncol], AF.Ln, [PSB(YB)], ["grs"], scale=1.0 / 128.0, bias=RMS_EPS)
                    ACT(grs[:, 0:ncol], grs[:, 0:ncol], AF.Exp, ["grs"], ["grs"], scale=-0.5)
                    TT(gon[:, 0:ncol], ps[:, zb, 0:ncol], grs[:, 0:ncol], ALU.mult, [PSB(zb), "grs"], ["gon"])
                    TT(dst, gon[:, 0:ncol].rearrange("p (h t) -> p h t", h=4), gv, ALU.mult, ["gon", "g2"], ["aT"])

                pieceA(0)
                yield
                for c in range(8):
                    if c > 0:
                        cp_ = c - 1
                        ccp = OC + 128 * cp_
                        pieceC(5 + cp_ % 2, 512, aT[:, 4 * H:4 * H + 4, ccp:ccp + 128], g2[:, :, ccp:ccp + 128], 4, 128)
                    pieceB(c)
                    if c + 1 < 8:
                        pieceA(c + 1)
                    yield
                ccp = OC + 128 * 7
                pieceC(5 + 7 % 2, 512, aT[:, 4 * H:4 * H + 4, ccp:ccp + 128], g2[:, :, ccp:ccp + 128], 4, 128)
                DMA("sp", s_p[4 * H:4 * H + 4].rearrange("h k v -> k h v"), Sf[:, hs, :], ("sp", H), reads=["Sf"])
                zb = 5
                def g1(e):
                    ins = None
                    for i in range(4):
                        ins = e.transpose(out=psb[:64, XB, i * 128:(i + 1) * 128], in_=ka[:, i, SC:SC + 64], identity=identb[:, :])
                    for i in range(4):
                        ins = e.matmul(ps[:64, YB, i * 64:(i + 1) * 64], lhsT=ka[:, i, SC:SC + 64], rhs=qa[:, i, SC:SC + 64],
                                       start=True, stop=True)
                    return ins
                PE(g1, [("ka", i) for i in range(4)] + [("qa", i) for i in range(4)] + ["identb"], [PSB(XB), PSB(YB)])
                ACT(gkatok[0][:64, :, :].rearrange("p h k -> p (h k)"), psb[:64, XB, 0:512], AF.Copy, [PSB(XB)], [("gk", 0)])
                ATs = gATm[0][:64, :, 0:64]
                TT(ATs, ps[:64, YB, 0:256].rearrange("p (h t) -> p h t", h=4),
                   smask[:64, :].rearrange("p (o t) -> p o t", o=1).broadcast_to([64, 4, 64]), ALU.mult, [PSB(YB), "cst"], [("ga", 0)])
                yield
                def g2f(e):
                    ins = None
                    for i in range(4):
                        ins = e.matmul(ps[:, zb, i * 64:(i + 1) * 64], lhsT=v_tok[:64, 8, i * 128:(i + 1) * 128], rhs=ATs[:, i, :],
                                       start=(i == 0), stop=False, skip_group_check=True)
                    return ins
                PE(g2f, [("ga", 0), "v"], [PSB(zb)])
                P.op("dve", lambda e: e.tensor_copy(out=ks_all[:64, 4 * H:4 * H + 4, :], in_=gkatok[0][:64, :, :]),
                     reads=[("gk", 0)], writes=["ks_all"])
                P.op("dve", lambda e: e.tensor_copy(out=vs_all[:64, 512 * H:512 * (H + 1)], in_=v_tok[:64, 8, 0:512]),
                     reads=["v"], writes=["vs_all"])
                for j in range(16):
                    sl = j % 4
                    DMA("sp", Sin[sl], sh[j, 4 * H:4 * H + 4].rearrange("h k v -> k h v"), ("sin", sl), writes=[("Sin", sl)])
                    P.op("act", lambda e, sl=sl: e.activation(out=Sjb[sl], in_=Sin[sl], func=AF.Copy), reads=[("Sin", sl)], writes=[("Sjb", sl)])
                    def g3(e, j=j, sl=sl):
                        ins = None
                        for i in range(4):
                            ins = e.matmul(ps[:, zb, i * 64 + 4 * j:i * 64 + 4 * j + 4], lhsT=Sjb[sl][:, i, :],
                                           rhs=qb[:, i, SC + 4 * j:SC + 4 * j + 4], start=False, stop=(j == 15), skip_group_check=True)
                        return ins
                    PE(g3, [("Sjb", sl)] + [("qb", i) for i in range(4)], [PSB(zb)])
                    if j % 8 == 7:
                        yield
                ACT(gsq[:, 0:256], ps[:, zb, 0:256], AF.Square, [PSB(zb)], ["gsq"])
                pieceC(zb, 256, aT[:, 4 * H:4 * H + 4, SC:SC + 64], g2[:, :, SC:SC + 64], 4, 64)

            def conv_block(j):
                (wv, wres), = ws.acquire(1)
                banks = fm_block(wv, wres, 0, xnT, "xT", OWN_GROUPS)
                per_group(OWN_GROUPS, banks, lambda src, c0, c1, pres: ACT(tt[0][:, c0:c1], src, AF.Copy, [pres], [("t", 0)]))
                tick()
                (wv, wres), = ws.acquire(1)
                banks = fm_block(wv, wres, 0, xnT, "xT", OWN_GROUPS)
                U = tt[1]
                Us = tt[2][:, 0:96].rearrange("p (s t) -> p s t", t=6)
                TT(U[:, 0:512], ps[:, banks[0], 0:512], tt[0][:, 0:512], ALU.mult, [PSB(banks[0]), ("t", 0)], [("t", 1)])
                TT(U[:, 512:1024], ps[:, banks[1], 0:512], tt[0][:, 512:1024], ALU.mult, [PSB(banks[1]), ("t", 0)], [("t", 1)])
                TT(U[:, 1024:1026], ps[:, banks[2], 0:2], tt[0][:, 1024:1026], ALU.mult, [PSB(banks[2]), ("t", 0)], [("t", 1)])
                TT(Us[:, :, 2:6], ps[:, banks[2], 2:66].rearrange("p (s t) -> p s t", t=4),
                   tt[0][:, SC:SC + 64].rearrange("p (s t) -> p s t", t=4), ALU.mult, [PSB(banks[2]), ("t", 0)], [("t", 2)])
                P.op("dve", lambda e: e.tensor_copy(out=Us[:, :, 0:2], in_=scT[:, j, :].rearrange("p (s t) -> p s t", t=2)),
                     reads=["scT"], writes=[("t", 2)])
                tick()
                cv = tt[3]
                TS(cv[:, 0:1024], U[:, 2:1026], cw[:, j, 2:3], ALU.mult, [("t", 1), "cols"], [("t", 3)])
                STT(cv[:, 0:1024], U[:, 1:1025], cw[:, j, 1:2], cv[:, 0:1024], ALU.mult, ALU.add, [("t", 1), ("t", 3), "cols"], [("t", 3)])
                STT(cv[:, 0:1024], U[:, 0:1024], cw[:, j, 0:1], cv[:, 0:1024], ALU.mult, ALU.add, [("t", 1), ("t", 3), "cols"], [("t", 3)])
                cvs = cv[:, 1024:1088].rearrange("p (s t) -> p s t", t=4)
                TS(cvs, Us[:, :, 2:6], cw[:, j, 2:3], ALU.mult, [("t", 2), "cols"], [("t", 3)])
                STT(cvs, Us[:, :, 1:5], cw[:, j, 1:2], cvs, ALU.mult, ALU.add, [("t", 2), ("t", 3), "cols"], [("t", 3)])
                STT(cvs, Us[:, :, 0:4], cw[:, j, 0:1], cvs, ALU.mult, ALU.add, [("t", 2), ("t", 3), "cols"], [("t", 3)])
                P.op("act", lambda e: e.activation(out=ulast[:, j, :], in_=U[:, 1024:1026], func=AF.Copy), reads=[("t", 1)], writes=["ulast"])
                P.op("act", lambda e: e.activation(out=ulast_s[:, j, :].rearrange("p (s t) -> p s t", t=2), in_=Us[:, :, 4:6], func=AF.Copy),
                     reads=[("t", 2)], writes=["ulast_s"])
                (wv, wres), = ws.acquire(1)
                banks = fm_block(wv, wres, 0, xnT, "xT", OWN_GROUPS)
                TT(aT[:, 8 + j, OC:OC + 510], ps[:, banks[0], 2:512], cv[:, 0:510], ALU.mult, [PSB(banks[0]), ("t", 3)], ["aT"])
                TT(aT[:, 8 + j, OC + 510:OC + 1022], ps[:, banks[1], 0:512], cv[:, 510:1022], ALU.mult, [PSB(banks[1]), ("t", 3)], ["aT"])
                TT(aT[:, 8 + j, OC + 1022:OC + 1024], ps[:, banks[2], 0:2], cv[:, 1022:1024], ALU.mult, [PSB(banks[2]), ("t", 3)], ["aT"])
                TT(aT[:, 8 + j, SC:SC + 64], ps[:, banks[2], 2:66], cv[:, 1024:1088], ALU.mult, [PSB(banks[2]), ("t", 3)], ["aT"])
                tick()

            for H in range(2):
                fm_mode[0] = "dbl"
                for i in range(4):
                    head_elementwise(H, i)
                units = ws.acquire(2)
                for ti, (c0, R) in enumerate(OWN_TILES):
                    v_tile(units, xnT, "xT", c0, R, v_tok, ti, 0)
                for u in range(2):
                    (wv, wres), = ws.acquire(1)
                    for hh in range(2):
                        i = 2 * u + hh
                        banks = fm_block(wv, wres, 128 * hh, xnT, "xT", OWN_GROUPS)
                        per_group(OWN_GROUPS, banks, lambda src, c0, c1, pres, i=i: ACT(
                            g2[:, i, c0:c1], src, AF.Silu, [pres], ["g2"]))
                        TS(g2[:, i, :], g2[:, i, :], gn[:, 4 * H + i:4 * H + i + 1], ALU.mult, ["g2", "cols"], ["g2"])
                fm_mode[0] = "low"
                bg.append(gla_half(H))
                tick()
                for j in range(4 * H, 4 * H + 4):
                    conv_block(j)
                drain()

            def fc1(e):
                ins = None
                for j in range(8):
                    ins = e.transpose(out=ps[:2, j // 4, (j % 4) * 128:(j % 4 + 1) * 128], in_=ulast[:, j, :], identity=ident)
                return ins
            PE(fc1, ["ulast", "cst"], [PSB(0), PSB(1)])
            ACT(cout2, ps[:2, 0:2, :].rearrange("p b c -> p (b c)"), AF.Copy, [PSB(0), PSB(1)], ["cout2"])
            DMA("sp", c_p, cout2, ("cp", 0), reads=["cout2"])
            def fc2(e):
                ins = None
                for j in range(8):
                    ins = e.transpose(out=ps[:32, 2 + j // 4, (j % 4) * 128:(j % 4 + 1) * 128], in_=ulast_s[:, j, :], identity=ident)
                return ins
            PE(fc2, ["ulast_s", "cst"], [PSB(2), PSB(3)])
            ACT(cout, ps[:32, 2:4, :].rearrange("p b c -> p (b c)"), AF.Copy, [PSB(2), PSB(3)], ["cout"])
            DMA("sp", c_s, cout, ("cs", 0), reads=["cout"])
            P.barrier()

            if _STOP < 3:
                raise _Skip()
            def load_ln(gv, bv, scale):
                DMA("sp", lnA, gv.partition_broadcast(128), ("ln", 0), writes=["lnA"])
                DMA("sp", lnB, bv.partition_broadcast(128), ("ln", 1), writes=["lnB"])
                if scale != 1.0:
                    TS(lnA, lnA, scale, ALU.mult, ["lnA"], ["lnA"])
                    TS(lnB, lnB, scale, ALU.mult, ["lnB"], ["lnB"])

            load_ln(ln0_g, ln0_b, ALPHA)
            xsrc = [xo[128 * j:128 * (j + 1), :] for j in range(8)] + [xs[0:64, :]]
            for t, (c0, R) in enumerate(OWN_TILES):
                DMA("sp", resid[t][:R], xsrc[t], ("rx", t % 4), writes=[("r", t)])
            s2c = [0]
            for n in range(8):
                (wv, wres), = ws.acquire(1)
                cs = slice(256 * n, 256 * (n + 1))
                for t, (c0, R) in enumerate(OWN_TILES):
                    rt = resid[t]
                    b = s2c[0] % 4; s2c[0] += 1
                    def f(e, c0=c0, R=R, b=b, wv=wv):
                        ins = None
                        for k in range(KC):
                            ins = e.matmul(ps[:R, b, 0:256], lhsT=aT[:, k, c0:c0 + R], rhs=wv[:, k, 0:256],
                                           start=(k == 0), stop=(k == KC - 1))
                        return ins
                    PE(f, [wres, "aT"], [PSB(b)])
                    ACT(rt[:R, cs], rt[:R, cs], AF.Identity, [("r", t), "mvs"], [("r", t)], scale=mvs[:R, t, 0:1], bias=mvs[:R, t, 1:2])
                    TT(rt[:R, cs], rt[:R, cs], lnA[:R, cs], ALU.mult, [("r", t), "lnA"], [("r", t)])
                    TT(rt[:R, cs], rt[:R, cs], lnB[:R, cs], ALU.add, [("r", t), "lnB"], [("r", t)])
                    TT(rt[:R, cs], rt[:R, cs], ps[:R, b, 0:256], ALU.add, [("r", t), PSB(b)], [("r", t)])
            load_ln(ln1_g, ln1_b, ALPHA)

            def ln1_tile(t, c0, R):
                rt = resid[t]
                sm, smres = stats_rstd(rt, R, ("r", t))
                ACT(rt[:R], rt[:R], AF.Identity, [("r", t), smres], [("r", t)], scale=sm[:R, 0:1], bias=sm[:R, 1:2])
                yield
                b0 = 4 * (t % 2)
                def f(e):
                    ins = None
                    for kb in range(KC):
                        ins = e.transpose(out=ps[:, b0 + kb // 4, (kb % 4) * 128:(kb % 4) * 128 + R],
                                          in_=rt[:R, kb * 128:(kb + 1) * 128], identity=ident[:R, :R])
                    return ins
                PE(f, [("r", t), "cst"], [PSB(b0 + i) for i in range(4)])
                yield
                for kb in range(KC):
                    src = ps[:, b0 + kb // 4, (kb % 4) * 128:(kb % 4) * 128 + R]
                    dst = hT[:, kb, c0:c0 + R]
                    if (kb // 4) != 3:
                        ACT(dst, src, AF.Identity, [PSB(b0 + kb // 4), "cols"], [("hT", 0)], scale=g1c[:, kb:kb + 1], bias=b1c[:, kb:kb + 1])
                    else:
                        TS(dst, src, g1c[:, kb:kb + 1], ALU.mult, [PSB(b0 + kb // 4), "cols"], [("hT", 1)], s2=b1c[:, kb:kb + 1], op1=ALU.add)
                TT(rt[:R], rt[:R], lnA[:R], ALU.mult, [("r", t), "lnA"], [("r", t)])
                TT(rt[:R], rt[:R], lnB[:R], ALU.add, [("r", t), "lnB"], [("r", t)], eng="pool")
            lg = [ln1_tile(t, c0, R) for t, (c0, R) in enumerate(OWN_TILES)]
            for s_ in range(len(lg) + 2):
                if s_ < len(lg):
                    next(lg[s_])
                if 0 <= s_ - 1 < len(lg):
                    next(lg[s_ - 1])
                if 0 <= s_ - 2 < len(lg):
                    run_gen(lg[s_ - 2])
            load_ln(ln2_g, ln2_b, 1.0)
            P.barrier()

            if _STOP < 4:
                raise _Skip()
            ws.grow([(arA[:, 4 * TO + i * SLOTW:4 * TO + (i + 1) * SLOTW], ("xslot", i)) for i in range(2)])
            ydst = [y_own[128 * j:128 * (j + 1), :] for j in range(8)] + [y_samp[:, :]]
            def ln2_a(t, R):
                rt = resid[t]
                sm, smres = stats_rstd(rt, R, ("r", t))
                ACT(rt[:R], rt[:R], AF.Identity, [("r", t), smres], [("r", t)], scale=sm[:R, 0:1], bias=sm[:R, 1:2])
            def ln2_b(t, R):
                rt = resid[t]
                TT(rt[:R], rt[:R], lnA[:R], ALU.mult, [("r", t), "lnA"], [("r", t)])
                TT(rt[:R], rt[:R], lnB[:R], ALU.add, [("r", t), "lnB"], [("r", t)], eng="pool")
                DMA("sp", ydst[t], rt[:R], ("yo", t), reads=[("r", t)])

            def samp_state():
                for j in range(16):
                    sl = j % 2
                    DMA("sp", Sin3[sl], sh[j].rearrange("h k v -> k h v"), ("sin3", sl), writes=[("Sin3", sl)])
                    TS(km3[:64], ks_all[:64, :, :], sel[:64, j:j + 1], ALU.mult, ["ks_all", "cst"], ["km3"])
                    def g4(e):
                        ins = None
                        for h in range(NH):
                            ins = e.matmul(ps[:, 6 + h // 4, (h % 4) * 128:(h % 4 + 1) * 128], lhsT=km3[:64, h, :],
                                           rhs=vs_all[:64, h * 128:(h + 1) * 128], start=True, stop=True)
                        return ins
                    PE(g4, ["km3", "vs_all"], [PSB(6), PSB(7)])
                    TT(Sin3[sl], Sin3[sl], C2s[:, :, j:j + 1].broadcast_to([128, NH, 128]), ALU.mult, [("Sin3", sl), "C2s"], [("Sin3", sl)])
                    TT(Sin3[sl], Sin3[sl], ps[:, 6:8, :].rearrange("p b (h v) -> p (b h) v", v=128), ALU.add,
                       [("Sin3", sl), PSB(6), PSB(7)], [("Sin3", sl)])
                    DMA("sp", s_s[j].rearrange("h k v -> k h v"), Sin3[sl], ("sout3", sl), reads=[("Sin3", sl)])
                    yield
                    yield
            bg.append(samp_state())
            gu = [0]; dn = [0]
            for g in range(11):
                for u in range(2):
                    (gw, gres), (uw, ures) = ws.acquire(2)
                    for bb in range(2):
                        blk = 2 * u + bb
                        for (c0, c1) in H_GROUPS:
                            n = c1 - c0
                            gb, ub = (0, 1) if gu[0] % 2 == 0 else (2, 3)
                            sl = gu[0] % 2; gu[0] += 1
                            def f(e, c0=c0, c1=c1, n=n, gb=gb, ub=ub, gw=gw, uw=uw, bb=bb):
                                ins = None
                                for k in range(KC):
                                    ins = e.matmul(ps[:, gb, 0:n], lhsT=gw[:, k, 128 * bb:128 * (bb + 1)], rhs=hT[:, k, c0:c1],
                                                   start=(k == 0), stop=(k == KC - 1))
                                for k in range(KC):
                                    ins = e.matmul(ps[:, ub, 0:n], lhsT=uw[:, k, 128 * bb:128 * (bb + 1)], rhs=hT[:, k, c0:c1],
                                                   start=(k == 0), stop=(k == KC - 1))
                                return ins
                            PE(f, [gres, ures, ("hT", 0), ("hT", 1)], [PSB(gb), PSB(ub)])
                            ACT(sgt[sl][:, 0:n], ps[:, gb, 0:n], AF.Silu, [PSB(gb)], [("sg", sl)])
                            TT(actT[:, blk, c0:c1], sgt[sl][:, 0:n], ps[:, ub, 0:n], ALU.mult, [("sg", sl), PSB(ub)], ["actT"])
                            tick()
                units = ws.acquire(2)
                def down(n, t, c0, R, units=units):
                    b = 4 + dn[0] % (2 if bg else 4); dn[0] += 1
                    def f(e):
                        ins = None
                        for kc in range(4):
                            wv = units[kc // 2][0]
                            ins = e.matmul(ps[:R, b, :], lhsT=actT[:, kc, c0:c0 + R], rhs=wv[:, kc % 2, 512 * n:512 * (n + 1)],
                                           start=(kc == 0), stop=(kc == 3))
                        return ins
                    PE(f, [units[0][1], units[1][1], "actT"], [PSB(b)])
                    TT(resid[t][:R, 512 * n:512 * (n + 1)], resid[t][:R, 512 * n:512 * (n + 1)], ps[:R, b, :], ALU.add,
                       [("r", t), PSB(b)], [("r", t)])
                if g < 10:
                    for n in range(4):
                        for t, (c0, R) in enumerate(OWN_TILES):
                            down(n, t, c0, R)
                else:
                    for t, (c0, R) in enumerate(OWN_TILES):
                        for n in range(4):
                            down(n, t, c0, R)
                        ln2_a(t, R)
                        if t > 0:
                            ln2_b(t - 1, OWN_TILES[t - 1][1])
                    ln2_b(8, OWN_TILES[8][1])
            P.barrier()
        except _Skip:
            P.barrier()
        P.op("sp", lambda e: e.nop(), reads=(), writes=())

        with nc.Block() as block:
            def emit(eng_name, e):
                for (waits, fn, semkey, inc) in P.streams[eng_name]:
                    for (k, v) in waits:
                        e.wait_ge(sems[k], v)
                    ins = fn(e)
                    ins.then_inc(sems[semkey], inc)

            @block.tensor
            def _(e):
                emit("pe", e)

            @block.scalar
            def _(e):
                emit("act", e)

            @block.vector
            def _(e):
                emit("dve", e)

            @block.gpsimd
            def _(e):
                emit("pool", e)

            @block.sync
            def _(e):
                emit("sp", e)
    return nc


_NC_CACHE = {}


def _consts(flagval):
    c = np.zeros((128, NCONST), np.float32)
    c[:, C_ID:C_ID + 128] = np.eye(128, dtype=np.float32)
    s = np.arange(128)[:, None]; t = np.arange(128)[None, :]
    c[:, C_MASK:C_MASK + 128] = (s <= t).astype(np.float32)
    s6 = np.arange(128)[:, None]; t6 = np.arange(64)[None, :]
    c[:, C_SMASK:C_SMASK + 64] = ((s6 // 4 == t6 // 4) & (s6 <= t6)).astype(np.float32)
    c[:, C_SEL:C_SEL + 16] = (np.arange(128)[:, None] // 4 == np.arange(16)[None, :]).astype(np.float32)
    c[:, C_FLAG] = flagval
    c[:, C_ONES:C_ONES + 128] = 1.0
    return c


def kernel(x_prompt, x_sample, state_hgrn, state_conv, meta_tokens, ln0_g, ln0_b,
           w_in, b_f, lb_param, gnorm_g, conv_w, w_o, ln1_g, ln1_b,
           w_gate, w_up, w_down, ln2_g, ln2_b):
    f = lambda a: np.ascontiguousarray(np.asarray(a), dtype=np.float32)
    x_prompt = f(x_prompt); x_sample = f(x_sample); state_hgrn = f(state_hgrn); state_conv = f(state_conv)
    meta = f(meta_tokens)
    shared = {
        "w_in": f(w_in)[0], "w_o": f(w_o)[0], "w_gate": f(w_gate)[0], "w_up": f(w_up)[0], "w_down": f(w_down)[0],
        "ln0_g": f(ln0_g), "ln0_b": f(ln0_b), "ln1_g": f(ln1_g)[0], "ln1_b": f(ln1_b)[0],
        "ln2_g": f(ln2_g)[0], "ln2_b": f(ln2_b)[0], "b_f": f(b_f)[0], "lb_param": f(lb_param),
        "gnorm_g": f(gnorm_g)[0], "conv_w": f(conv_w)[0],
    }
    vecs_host = np.ascontiguousarray(np.concatenate([
        shared["b_f"].reshape(8, 128), shared["gnorm_g"].reshape(8, 128),
        shared["ln0_g"].reshape(16, 128), shared["ln0_b"].reshape(16, 128),
        shared["ln1_g"].reshape(16, 128), shared["ln1_b"].reshape(16, 128),
        shared["conv_w"].reshape(24, 128), shared["lb_param"].reshape(16, 128)], axis=0))
    in_maps = []
    for c in range(8):
        seq, half = c // 2, c % 2
        xo = x_prompt[seq, 1024 * half:1024 * (half + 1)]
        xp = np.zeros((TP, D), np.float32)
        xp[:16] = meta
        if half == 1:
            xp[16:] = x_prompt[seq, 0:1024]
            halo = x_prompt[seq, 1022:1024]
        else:
            halo = meta[14:16]
        xs = np.concatenate([x_sample[16 * c:16 * (c + 1)].reshape(64, D), halo], axis=0)
        m = {k: v for k, v in shared.items() if k not in ("b_f", "lb_param", "gnorm_g", "conv_w")}
        m.update({
            "xo": np.ascontiguousarray(xo), "xs": np.ascontiguousarray(xs), "xp": xp,
            "sh": np.ascontiguousarray(state_hgrn[0, 16 * c:16 * (c + 1)]),
            "sc": np.ascontiguousarray(state_conv[0, 16 * c:16 * (c + 1)].reshape(32, 1024)),
            "consts": _consts(float(half)),
            "vecs": vecs_host,
        })
        in_maps.append(m)
    if "nc" not in _NC_CACHE:
        _NC_CACHE["nc"] = build_nc()
    nc = _NC_CACHE["nc"]
    res = run_bass_kernel_spmd(nc, in_maps, core_ids=list(range(8)))
    R = res.results
    y_prompt = np.zeros((4, 2048, D), np.float32)
    y_sample = np.zeros((128, 4, D), np.float32)
    hp = np.zeros((1, 4, NH, 128, 128), np.float32)
    cp = np.zeros((1, 4, 2, 1024), np.float32)
    hs = np.zeros((1, 128, NH, 128, 128), np.float32)
    cs = np.zeros((1, 128, 2, 1024), np.float32)
    for c in range(8):
        seq, half = c // 2, c % 2
        y_prompt[seq, 1024 * half:1024 * (half + 1)] = R[c]["y_own"]
        y_sample[16 * c:16 * (c + 1)] = R[c]["y_samp"].reshape(16, 4, D)
        hs[0, 16 * c:16 * (c + 1)] = R[c]["s_s"]
        cs[0, 16 * c:16 * (c + 1)] = R[c]["c_s"].reshape(16, 2, 1024)
        if half == 1:
            hp[0, seq] = R[c]["s_p"]
            cp[0, seq] = R[c]["c_p"]
    return (y_prompt, y_sample, hp, cp, hs, cs)
```
